# Optimizing a Trainium2 kernel written in Bass

```python
import jax, jax.numpy as jnp
from jax import lax
import numpy as np

D_MODEL = 2048
BATCH = 2
SEQ = 8192
DEPTH = 4

HEAD_DIM = 128
A_HEADS = 8
A_KV_HEADS = 2
A_GROUP = A_HEADS // A_KV_HEADS
B_HEADS = 8
D_A = A_HEADS * HEAD_DIM
D_B = B_HEADS * HEAD_DIM
KV_DIM = A_KV_HEADS * HEAD_DIM
D_MIX = D_A + D_B
D_IN = 2 * D_A + 2 * KV_DIM + 4 * D_B
GRID_W = 64
AXIS_DIM = HEAD_DIM // 2
ROPE_THETA = 10000.0
Q_BLOCK = 128
DILATED_PATTERNS = ((128, 1), (512, 4), (2048, 16))
MAX_REACH = 1024
ALIBI_SLOPES = (2.0 ** -np.arange(1, B_HEADS + 1)).astype(np.float32)
SCALE = HEAD_DIM ** -0.5
EPS = 1e-6
NEG_INF = -1e30

kernel_name = 'hymba_axial_gqa_dilated_encoder'


def rms_norm(x, w):
    xf = x.astype(jnp.float32)
    y = xf * lax.rsqrt(jnp.mean(xf * xf, axis=-1, keepdims=True) + EPS)
    return (y * w.astype(jnp.float32)).astype(x.dtype)


def axial_rope_tables(T):
    rows = T // GRID_W
    row = jnp.broadcast_to(jnp.arange(rows, dtype=jnp.float32)[:, None], (rows, GRID_W)).reshape(-1)
    col = jnp.broadcast_to(jnp.arange(GRID_W, dtype=jnp.float32)[None, :], (rows, GRID_W)).reshape(-1)
    inv_freq = ROPE_THETA ** (-jnp.arange(0, AXIS_DIM, 2, dtype=jnp.float32) / AXIS_DIM)
    ang_r = row[:, None] * inv_freq[None, :]
    ang_c = col[:, None] * inv_freq[None, :]
    return (jnp.cos(ang_r), jnp.sin(ang_r), jnp.cos(ang_c), jnp.sin(ang_c))


def rotate_axis(x, cos, sin):
    x1, x2 = jnp.split(x, 2, axis=-1)
    return jnp.concatenate([x1 * cos - x2 * sin, x1 * sin + x2 * cos], axis=-1)


def apply_axial_rope(x, tables):
    cr, sr, cc, sc = tables
    xf = x.astype(jnp.float32)
    out = jnp.concatenate([rotate_axis(xf[..., :AXIS_DIM], cr, sr),
                           rotate_axis(xf[..., AXIS_DIM:], cc, sc)], axis=-1)
    return out.astype(x.dtype)


def axial_gqa(q, k, v, q_gain, k_gain, tables):
    B_, T = q.shape[0], q.shape[1]
    nb = T // Q_BLOCK
    q = apply_axial_rope(rms_norm(q, q_gain).transpose(0, 2, 1, 3), tables)
    k = apply_axial_rope(rms_norm(k, k_gain).transpose(0, 2, 1, 3), tables)
    v = v.transpose(0, 2, 1, 3)
    qb = jnp.moveaxis(q.reshape(B_, A_KV_HEADS, A_GROUP, nb, Q_BLOCK, HEAD_DIM), 3, 0)

    def one_block(qblk):
        s = jnp.einsum('bkgqd,bksd->bkgqs', qblk, k, preferred_element_type=jnp.float32) * SCALE
        p = jax.nn.softmax(s, axis=-1).astype(v.dtype)
        return jnp.einsum('bkgqs,bksd->bkgqd', p, v)

    o = lax.map(one_block, qb)
    o = jnp.moveaxis(o, 0, 3).reshape(B_, A_HEADS, T, HEAD_DIM)
    return o.transpose(0, 2, 1, 3).reshape(B_, T, D_A)


def dilated_attention(q, k, v):
    B_, T = q.shape[0], q.shape[1]
    nb = T // Q_BLOCK
    q = q.transpose(0, 2, 1, 3)
    pad = ((0, 0), (0, 0), (MAX_REACH, MAX_REACH), (0, 0))
    kp = jnp.pad(k.transpose(0, 2, 1, 3), pad)
    vp = jnp.pad(v.transpose(0, 2, 1, 3), pad)
    span = Q_BLOCK + 2 * MAX_REACH
    qb = jnp.moveaxis(q.reshape(B_, B_HEADS, nb, Q_BLOCK, HEAD_DIM), 2, 0)

    patterns = []
    for w, d in DILATED_PATTERNS:
        n = w // (2 * d)
        offs = (np.arange(-n, n + 1) * d).astype(np.int32)
        idx = np.arange(Q_BLOCK, dtype=np.int32)[:, None] + MAX_REACH + offs[None, :]
        bias = -ALIBI_SLOPES[:, None, None] * np.abs(offs).astype(np.float32)[None, None, :]
        patterns.append((offs, idx, bias))

    def one_block(args):
        blk, qblk = args
        t0 = blk * Q_BLOCK
        kc = lax.dynamic_slice_in_dim(kp, t0, span, axis=2)
        vc = lax.dynamic_slice_in_dim(vp, t0, span, axis=2)
        outs, lses = [], []
        for offs, idx, bias in patterns:
            kg = kc[:, :, idx, :]
            vg = vc[:, :, idx, :]
            pos = t0 + jnp.arange(Q_BLOCK)[:, None] + offs[None, :]
            valid = (pos >= 0) & (pos < T)
            s = jnp.einsum('bhqd,bhqjd->bhqj', qblk, kg, preferred_element_type=jnp.float32) * SCALE + bias
            s = jnp.where(valid, s, NEG_INF)
            lse = jax.nn.logsumexp(s, axis=-1, keepdims=True)
            p = jnp.exp(s - lse).astype(vg.dtype)
            outs.append(jnp.einsum('bhqj,bhqjd->bhqd', p, vg, preferred_element_type=jnp.float32))
            lses.append(lse)
        wts = jax.nn.softmax(jnp.stack(lses, axis=0), axis=0)
        o = jnp.sum(wts * jnp.stack(outs, axis=0), axis=0)
        return o.astype(qblk.dtype)

    o = lax.map(one_block, (jnp.arange(nb), qb))
    o = jnp.moveaxis(o, 0, 2).reshape(B_, B_HEADS, T, HEAD_DIM)
    return o.transpose(0, 2, 1, 3).reshape(B_, T, D_B)


def setup_inputs(seed: int = 0) -> dict:
    key = jax.random.key(seed)
    ks = jax.random.split(key, 9)
    f32 = jnp.float32
    x = jax.random.normal(ks[0], (BATCH, SEQ, D_MODEL), f32)
    norm_w = 1.0 + 0.02 * jax.random.normal(ks[1], (DEPTH, D_MODEL), f32)
    w_in = jax.random.normal(ks[2], (DEPTH, D_MODEL, D_IN), f32) * D_MODEL ** -0.5
    q_norm_a = 1.0 + 0.02 * jax.random.normal(ks[3], (DEPTH, HEAD_DIM), f32)
    k_norm_a = 1.0 + 0.02 * jax.random.normal(ks[4], (DEPTH, HEAD_DIM), f32)
    out_norm_a = 1.0 + 0.02 * jax.random.normal(ks[5], (DEPTH, D_A), f32)
    out_norm_b = 1.0 + 0.02 * jax.random.normal(ks[6], (DEPTH, D_B), f32)
    w_out = jax.random.normal(ks[7], (DEPTH, D_MIX, D_MODEL), f32) * D_MIX ** -0.5
    final_norm = 1.0 + 0.02 * jax.random.normal(ks[8], (D_MODEL,), f32)
    return {'x': x, 'norm_w': norm_w, 'w_in': w_in, 'q_norm_a': q_norm_a, 'k_norm_a': k_norm_a,
            'out_norm_a': out_norm_a, 'out_norm_b': out_norm_b, 'w_out': w_out, 'final_norm': final_norm}


def reference(x, norm_w, w_in, q_norm_a, k_norm_a, out_norm_a, out_norm_b, w_out, final_norm):
    B_, T, _ = x.shape
    tables = axial_rope_tables(T)
    splits = [int(s) for s in np.cumsum([D_A, KV_DIM, KV_DIM, D_A, D_B, D_B, D_B])]
    for l in range(DEPTH):
        h = rms_norm(x, norm_w[l])
        proj = h @ w_in[l]
        q_a, k_a, v_a, g_a, q_b, k_b, v_b, g_b = jnp.split(proj, splits, axis=-1)
        y_a = axial_gqa(q_a.reshape(B_, T, A_HEADS, HEAD_DIM),
                        k_a.reshape(B_, T, A_KV_HEADS, HEAD_DIM),
                        v_a.reshape(B_, T, A_KV_HEADS, HEAD_DIM),
                        q_norm_a[l], k_norm_a[l], tables)
        y_b = dilated_attention(q_b.reshape(B_, T, B_HEADS, HEAD_DIM),
                                k_b.reshape(B_, T, B_HEADS, HEAD_DIM),
                                v_b.reshape(B_, T, B_HEADS, HEAD_DIM))
        y = jnp.concatenate([rms_norm(y_a, out_norm_a[l]) * jax.nn.silu(g_a),
                             rms_norm(y_b, out_norm_b[l]) * jax.nn.silu(g_b)], axis=-1)
        x = x + y @ w_out[l]
    return rms_norm(x, final_norm)
```

```python
import contextlib
import numpy as np
import ml_dtypes
import concourse.bass as bass
import concourse.mybir as mybir
from concourse.bass_utils import run_bass_kernel_spmd

F32 = mybir.dt.float32
BF16 = mybir.dt.bfloat16
ALU = mybir.AluOpType
AF = mybir.ActivationFunctionType
AX = mybir.AxisListType

ENGS = ["pe", "act", "dve", "pool", "sp"]

D = 2048
SEQ = 8192
TOK = 2048
NT = TOK // 128
DIN = 6656
HD = 128
EPS = 1e-6
SCALE = HD ** -0.5
RANK = 2560 * 2048
KA_OFF = 0
VA_OFF = 256 * 2048
KB_OFF = 512 * 2048
VB_OFF = 1536 * 2048
PATTERNS = ((128, 1), (512, 4), (2048, 16))


class Prog:
    def __init__(self, nc, n_dma_sems=110):
        self.nc = nc
        self.segments = []
        self.cur = {e: [] for e in ENGS}
        self.cnt = {e: 0 for e in ENGS}
        self.waited = {e: {} for e in ENGS}
        self.bufs = {}
        self.dtot = {}
        self.n_dma_sems = n_dma_sems
        self.dma_keys = {}
        self.nbar = 0
        self.same_eng_wait = {"pe": False, "act": True, "dve": True, "pool": True, "sp": False}

    def _deps(self, eng, reads, writes):
        evs = []
        for k in reads:
            b = self.bufs.get(k)
            if b and b[0] is not None:
                evs.append(b[0])
        for k in writes:
            b = self.bufs.get(k)
            if b:
                if b[0] is not None:
                    evs.append(b[0])
                evs.extend(b[1])
        need = {}
        for (sk, v) in evs:
            if sk == eng and not self.same_eng_wait[eng]:
                continue
            if self.waited[eng].get(sk, 0) >= v:
                continue
            if need.get(sk, 0) < v:
                need[sk] = v
        for sk, v in need.items():
            self.waited[eng][sk] = v
        return list(need.items())

    def _record(self, ev, reads, writes):
        for k in reads:
            self.bufs.setdefault(k, [None, []])[1].append(ev)
        for k in writes:
            self.bufs[k] = [ev, []]

    def op(self, eng, fn, reads=(), writes=()):
        waits = self._deps(eng, reads, writes)
        self.cnt[eng] += 1
        ev = (eng, self.cnt[eng])
        self.cur[eng].append(("op", fn, waits, ev))
        self._record(ev, reads, writes)
        return ev

    def dma(self, eng, semkey, fn, reads=(), writes=()):
        if semkey not in self.dma_keys:
            assert len(self.dma_keys) < self.n_dma_sems, "out of dma sems"
            self.dma_keys[semkey] = len(self.dma_keys)
        waits = self._deps(eng, reads, writes)
        self.dtot[semkey] = self.dtot.get(semkey, 0) + 16
        ev = (semkey, self.dtot[semkey])
        self.cur[eng].append(("dma", fn, waits, ev))
        self._record(ev, reads, writes)
        return ev

    def cc(self, fn):
        semkey = "_cc"
        if semkey not in self.dma_keys:
            self.dma_keys[semkey] = len(self.dma_keys)
        self.dtot[semkey] = self.dtot.get(semkey, 0) + 1
        ev = (semkey, self.dtot[semkey])
        self.cur["pool"].append(("cc", fn, [], ev))

    def barrier(self):
        self.nbar += 1
        self.cur["_bar"] = dict(cnt=dict(self.cnt), dtot=dict(self.dtot), idx=self.nbar)
        self.segments.append(self.cur)
        self.cur = {e: [] for e in ENGS}
        self.bufs = {}

    def emit(self):
        nc = self.nc
        if any(self.cur[e] for e in ENGS):
            self.barrier()
        segs = self.segments
        with contextlib.ExitStack() as st:
            esem = {e: st.enter_context(nc.semaphore("s_" + e)) for e in ENGS}
            dsem = [st.enter_context(nc.semaphore("d%d" % i)) for i in range(len(self.dma_keys))]
            b1 = st.enter_context(nc.semaphore("bar1"))
            b2 = st.enter_context(nc.semaphore("bar2"))
            block = st.enter_context(nc.Block())

            def sem_of(sk):
                if sk in esem:
                    return esem[sk]
                return dsem[self.dma_keys[sk]]

            def run(eng_name, e):
                for seg in segs:
                    for (kind, fn, waits, ev) in seg[eng_name]:
                        for (sk, v) in waits:
                            e.wait_ge(sem_of(sk), v)
                        ins = fn(e)
                        if kind == "op":
                            ins.then_inc(esem[eng_name], 1)
                        elif kind == "dma":
                            ins.then_inc(sem_of(ev[0]), 16)
                        elif kind == "cc":
                            ins.then_inc(sem_of(ev[0]), 1)
                    bar = seg["_bar"]
                    if bar["cnt"][eng_name] > 0:
                        e.wait_ge(esem[eng_name], bar["cnt"][eng_name])
                    if eng_name != "pool":
                        e.sem_inc(b1, 1)
                        e.wait_ge(b2, bar["idx"])
                    else:
                        e.wait_ge(b1, (len(ENGS) - 1) * bar["idx"])
                        for sk, v in bar["dtot"].items():
                            e.wait_ge(sem_of(sk), v)
                        e.sem_inc(b2, 1)

            @block.tensor
            def _(e):
                run("pe", e)

            @block.scalar
            def _(e):
                run("act", e)

            @block.vector
            def _(e):
                run("dve", e)

            @block.gpsimd
            def _(e):
                self.pool_eng_setup(e)
                run("pool", e)

            @block.sync
            def _(e):
                run("sp", e)

    def pool_eng_setup(self, e):
        pass


def build_program(nl, first, last):
    nc = bass.Bass("TRN2", target_bir_lowering=False)
    dt_in = lambda name, shape, dt=F32: nc.dram_tensor(name, shape, dt, kind="ExternalInput")
    x_in = dt_in("x", [TOK, D])
    w_in = dt_in("w_in", [nl, D, DIN])
    w_out = dt_in("w_out", [nl, D, D])
    norm_w = dt_in("norm_w", [nl, D])
    qn = dt_in("q_norm_a", [nl, HD])
    kn = dt_in("k_norm_a", [nl, HD])
    ona = dt_in("out_norm_a", [nl, 1024])
    onb = dt_in("out_norm_b", [nl, 1024])
    fnw = dt_in("final_norm", [1, D])
    ropeC = dt_in("ropeC", [TOK, HD])
    ropeS = dt_in("ropeS", [TOK, HD])
    maskB = dt_in("maskB", [128, 8 * 17 * 128], BF16)
    edge = dt_in("edge", [128, 2])
    ident_d = dt_in("ident", [128, 128], BF16)
    out_d = nc.dram_tensor("out", [TOK, D], F32, kind="ExternalOutput")

    xres = [nc.dram_tensor("xres%d" % i, [TOK, D], F32) for i in range(2)]
    qaT = nc.dram_tensor("qaT", [8, 128, TOK], BF16)
    qbT = nc.dram_tensor("qbT", [8, 128, TOK], BF16)
    gT = nc.dram_tensor("gT", [16, 128, TOK], BF16)
    yT = nc.dram_tensor("yT", [16, 128, TOK], F32)
    kvown = nc.dram_tensor("kvown", [2560, 2048], BF16)
    kvbig = nc.dram_tensor("kvbig", [10 * 1536, 2048], BF16)
    kvwin = nc.dram_tensor("kvwin", [8 * 768, 2048], BF16)

    P = Prog(nc)
    jreg = {}

    def setup_pool(e):
        pid = e.partition_id()
        jreg["j"] = pid % 4
    P.pool_eng_setup = setup_pool

    with contextlib.ExitStack() as gst:
        ident = gst.enter_context(nc.sbuf_tensor("ident_sb", [128, 128], BF16))
        ones = gst.enter_context(nc.sbuf_tensor("ones", [128, 128], BF16))
        onesf = gst.enter_context(nc.sbuf_tensor("onesf", [128, 128], F32))
        zer = gst.enter_context(nc.sbuf_tensor("zer", [128, 2048], BF16))
        P.dma("sp", "ident", lambda e: e.dma_start(out=ident[:], in_=ident_d.ap()), writes=["ident"])
        epsb = gst.enter_context(nc.sbuf_tensor("epsb", [128, 1], F32))
        P.op("dve", lambda e: e.memset(epsb[:], EPS), writes=["epsb"])
        P.op("dve", lambda e: e.memset(ones[:], 1.0), writes=["ones"])
        P.op("dve", lambda e: e.memset(onesf[:], 1.0), writes=["onesf"])
        P.op("dve", lambda e: e.memset(zer[:], 0.0), writes=["zer"])
        for ch in range(10):
            for slot in (0, 5):
                for part in range(2):
                    r0 = ch * 1536 + slot * 256 + part * 128
                    P.dma("sp", "zpad", lambda e, r0=r0: e.dma_start(out=kvbig.ap()[r0:r0 + 128, :], in_=zer[:]), reads=["zer"])
        P.barrier()

        for li in range(nl):
            x_src = x_in if (li == 0 and first) else xres[(li - 1) % 2]
            if li == 0 and not first:
                x_src = x_in
            is_last = (li == nl - 1)
            x_dst = out_d if is_last else xres[li % 2]
            layer(P, nc, li, x_src, x_dst, is_last and last, locals())
        P.emit()
    return nc


def layer(P, nc, li, x_src, x_dst, do_final, G):
    def psum_f32(stk, name, n):
        tl = [stk.enter_context(nc.psum_tensor(L + name + str(i), [128, 1024], F32)) for i in range(n)]
        halves = [tl[i // 2][:, (i % 2) * 512:(i % 2 + 1) * 512] for i in range(2 * n)]
        return tl, halves
    ident, ones, onesf, epsb = G["ident"], G["ones"], G["onesf"], G["epsb"]
    w_in, w_out, norm_w, qn, kn, ona, onb, fnw = (G[k] for k in ["w_in", "w_out", "norm_w", "qn", "kn", "ona", "onb", "fnw"])
    ropeC, ropeS, maskB, edge = G["ropeC"], G["ropeS"], G["maskB"], G["edge"]
    qaT, qbT, gT, yT, kvown, kvbig = G["qaT"], G["qbT"], G["gT"], G["yT"], G["kvown"], G["kvbig"]
    kvwin = G["kvwin"]
    jreg = G["jreg"]
    L = "L%d" % li

    def bcast_row(handle, row, n):
        return bass.AP(handle, row * n, [[0, 128], [1, n]])

    with contextlib.ExitStack() as st:
        sb = lambda name, shape, dt: st.enter_context(nc.sbuf_tensor(L + name, shape, dt))
        hT = sb("hT", [128, 16, TOK], BF16)
        with contextlib.ExitStack() as st1:
            sb1 = lambda name, shape, dt: st1.enter_context(nc.sbuf_tensor(L + name, shape, dt))
            xt = [sb1("xt%d" % i, [128, D], F32) for i in range(2)]
            hbt = [sb1("hbt%d" % i, [128, D], BF16) for i in range(2)]
            sq = sb1("sq", [128, D], F32)
            ss = sb1("ss", [128, 16], F32)
            rs = sb1("rs", [128, 16], F32)
            nw = sb1("nw", [128, D], F32)
            tpp = [st1.enter_context(nc.psum_tensor(L + "tpp%d" % i, [128, 1024], BF16)) for i in range(2)]
            P.dma("sp", "nw", lambda e: e.dma_start(out=nw[:], in_=bcast_row(norm_w, li, D)), writes=["nw"])
            for t in range(NT):
                s = t % 2
                P.dma("sp", "xt%d" % s, lambda e, t=t, s=s: e.dma_start(out=xt[s][:], in_=x_src.ap()[t * 128:(t + 1) * 128, :]), writes=["xt%d" % s])
                P.op("act", lambda e, t=t, s=s: e.activation(out=sq[:], in_=xt[s][:], func=AF.Square, accum_out=ss[:, t:t + 1]),
                     reads=["xt%d" % s], writes=["sq", "ss%d" % t])
                P.op("act", lambda e, t=t: e.activation(out=rs[:, t:t + 1], in_=ss[:, t:t + 1], func=AF.Sqrt, scale=1.0 / D, bias=epsb[:]),
                     reads=["ss%d" % t, "epsb"], writes=["rs%d" % t])
                P.op("dve", lambda e, t=t: e.reciprocal(out=rs[:, t:t + 1], in_=rs[:, t:t + 1]),
                     reads=["rs%d" % t], writes=["rs%d" % t])
                P.op("dve", lambda e, t=t, s=s: e.scalar_tensor_tensor(out=hbt[s][:], in0=xt[s][:], scalar=rs[:, t:t + 1], in1=nw[:], op0=ALU.mult, op1=ALU.mult),
                     reads=["xt%d" % s, "rs%d" % t, "nw"], writes=["hbt%d" % s])
                for half in range(2):
                    pt = tpp[half][:]
                    for c in range(8):
                        P.op("pe", lambda e, s=s, half=half, c=c, pt=pt: e.transpose(out=pt[:, c * 128:(c + 1) * 128], in_=hbt[s][:, (half * 8 + c) * 128:(half * 8 + c + 1) * 128], identity=ident[:]),
                             reads=["hbt%d" % s, "ident"], writes=["tp%d" % half])
                    eng = "act" if half == 0 else "dve"
                    if eng == "act":
                        P.op("act", lambda e, t=t, half=half, pt=pt: e.activation(out=hT[:, half * 8:(half + 1) * 8, t * 128:(t + 1) * 128], in_=pt.rearrange("p (c k) -> p c k", k=128), func=AF.Copy),
                             reads=["tp%d" % half], writes=["hT"])
                    else:
                        P.op("dve", lambda e, t=t, half=half, pt=pt: e.tensor_copy(out=hT[:, half * 8:(half + 1) * 8, t * 128:(t + 1) * 128], in_=pt.rearrange("p (c k) -> p c k", k=128)),
                             reads=["tp%d" % half], writes=["hT"])
            P.barrier()

        wbuf = [sb("wb%d" % i, [128, 16, 512], BF16) for i in range(2)]
        _, hb_ = psum_f32(st, "acc", 2)
        tpq = st.enter_context(nc.psum_tensor(L + "tpq", [128, 512], BF16))
        Ct = sb("Ct", [128, NT, HD], F32)
        St = sb("St", [128, NT, HD], F32)
        gq = sb("gq", [128, HD], F32)
        gk = sb("gk", [128, HD], F32)
        junk = sb("junk", [128, HD], F32)
        ssq = sb("ssq", [128, 4], F32)
        rsq = sb("rsq", [128, 4], F32)
        xn = [sb("xn%d" % i, [128, HD], F32) for i in range(2)]
        t1 = [sb("t1%d" % i, [128, HD], F32) for i in range(2)]
        t2 = [sb("t2%d" % i, [128, HD], F32) for i in range(2)]
        ro = [sb("ro%d" % i, [128, HD], BF16) for i in range(2)]
        qstage = [sb("qst%d" % i, [128, 4, 512], BF16) for i in range(2)]
        fstage = [sb("fst%d" % i, [128, 4, 512], BF16) for i in range(2)]
        vstage = [sb("vst%d" % i, [128, 512], BF16) for i in range(2)]
        P.dma("sp", "Ct", lambda e: e.dma_start(out=Ct[:], in_=ropeC.ap().rearrange("(t p) d -> p t d", p=128)), writes=["Ct"])
        P.dma("sp", "St", lambda e: e.dma_start(out=St[:], in_=ropeS.ap().rearrange("(t p) d -> p t d", p=128)), writes=["St"])
        P.dma("sp", "gq", lambda e: e.dma_start(out=gq[:], in_=bcast_row(qn, li, HD)), writes=["gq"])
        P.dma("sp", "gk", lambda e: e.dma_start(out=gk[:], in_=bcast_row(kn, li, HD)), writes=["gk"])

        T_GROUPS = {0, 1, 2, 9, 10}
        acc_i = [0]
        tp_i = [0]
        hd_i = [0]

        def next_acc():
            a = acc_i[0] % 4
            acc_i[0] += 1
            return a

        def rope_head(acc_ap, accs_key, t, gain, out_stage, out_key, hslot, tcol):
            i = hd_i[0] % 2
            hd_i[0] += 1
            k = "h%d" % i
            P.op("act", lambda e: e.activation(out=junk[:], in_=acc_ap, func=AF.Square, accum_out=ssq[:, i:i + 1]),
                 reads=[accs_key], writes=["junk", "ssq" + k])
            P.op("act", lambda e: e.activation(out=rsq[:, i:i + 1], in_=ssq[:, i:i + 1], func=AF.Sqrt, scale=1.0 / HD, bias=epsb[:]),
                 reads=["ssq" + k, "epsb"], writes=["rsq" + k])
            P.op("dve", lambda e: e.reciprocal(out=rsq[:, i:i + 1], in_=rsq[:, i:i + 1]),
                 reads=["rsq" + k], writes=["rsq" + k])
            P.op("dve", lambda e: e.scalar_tensor_tensor(out=xn[i][:], in0=acc_ap, scalar=rsq[:, i:i + 1], in1=gain[:], op0=ALU.mult, op1=ALU.mult),
                 reads=[accs_key, "rsq" + k, "gq", "gk"], writes=["xn" + k])
            P.op("dve", lambda e: e.tensor_tensor(out=t1[i][:], in0=xn[i][:], in1=Ct[:, t, :], op=ALU.mult),
                 reads=["xn" + k, "Ct"], writes=["t1" + k])
            xv = xn[i][:].rearrange("p (a h c) -> p a h c", a=2, h=2)
            sv = St[:, t, :].rearrange("p (a h c) -> p a h c", a=2, h=2)
            tv = t2[i][:].rearrange("p (a h c) -> p a h c", a=2, h=2)
            P.op("dve", lambda e: e.tensor_tensor(out=tv[:, :, 0, :], in0=xv[:, :, 1, :], in1=sv[:, :, 0, :], op=ALU.mult),
                 reads=["xn" + k, "St"], writes=["t2a" + k])
            P.op("dve", lambda e: e.tensor_tensor(out=tv[:, :, 1, :], in0=xv[:, :, 0, :], in1=sv[:, :, 1, :], op=ALU.mult),
                 reads=["xn" + k, "St"], writes=["t2b" + k])
            P.op("dve", lambda e: e.tensor_tensor(out=ro[i][:], in0=t1[i][:], in1=t2[i][:], op=ALU.add),
                 reads=["t1" + k, "t2a" + k, "t2b" + k], writes=["ro" + k])
            tpi = tp_i[0] % 4
            tp_i[0] += 1
            pt = tpq[:, tpi * 128:(tpi + 1) * 128]
            P.op("pe", lambda e: e.transpose(out=pt, in_=ro[i][:], identity=ident[:]),
                 reads=["ro" + k, "ident"], writes=["tpq%d" % tpi])
            P.op("act", lambda e: e.activation(out=out_stage[:, hslot, tcol * 128:(tcol + 1) * 128], in_=pt, func=AF.Copy),
                 reads=["tpq%d" % tpi], writes=[out_key])

        for g in range(13):
            s = g % 2
            P.dma("pool", "wb%d" % s, lambda e, g=g, s=s: e.dma_start(out=wbuf[s][:], in_=w_in.ap()[li, :, g * 512:(g + 1) * 512].rearrange("(c p) n -> p c n", p=128)),
                  writes=["wb%d" % s])
            if g in T_GROUPS:
                for t in range(NT):
                    a = next_acc()
                    acc = hb_[a]
                    for c in range(16):
                        P.op("pe", lambda e, acc=acc, c=c, t=t, s=s: e.matmul(acc, lhsT=hT[:, c, t * 128:(t + 1) * 128], rhs=wbuf[s][:, c, :], start=(c == 0), stop=(c == 15)),
                             reads=["wb%d" % s], writes=["acc%d" % a])
                    tg, tc = t // 4, t % 4
                    if g in (0, 1):
                        qs = (g * 4 + tg) % 2
                        for hh in range(4):
                            rope_head(acc[:, hh * 128:(hh + 1) * 128], "acc%d" % a, t, gq, qstage[qs], "qst%d" % qs, hh, tc)
                        if tc == 3:
                            P.dma("sp", "qst%d" % qs, lambda e, g=g, tg=tg, qs=qs: e.dma_start(
                                out=qaT.ap()[g * 4:(g + 1) * 4, :, tg * 512:(tg + 1) * 512].rearrange("h d t -> d h t"), in_=qstage[qs][:]),
                                reads=["qst%d" % qs])
                    elif g == 2:
                        qs = tg % 2
                        for hh in range(2):
                            rope_head(acc[:, hh * 128:(hh + 1) * 128], "acc%d" % a, t, gk, qstage[qs], "qst%d" % qs, hh, tc)
                        if tc == 3:
                            P.dma("sp", "qst%d" % qs, lambda e, tg=tg, qs=qs: e.dma_start(
                                out=kvown.ap()[0:256, tg * 512:(tg + 1) * 512].rearrange("(h d) t -> d h t", h=2), in_=qstage[qs][:, 0:2, :]),
                                reads=["qst%d" % qs])
                        vs = t % 2
                        P.op("dve", lambda e, acc=acc, vs=vs: e.tensor_copy(out=vstage[vs][:, 0:256], in_=acc[:, 256:512]),
                             reads=["acc%d" % a], writes=["vst%d" % vs])
                        P.dma("sp", "vst%d" % vs, lambda e, t=t, vs=vs: e.dma_start(
                            out=bass.AP(kvown, VA_OFF + t * 128 * 256, [[256, 128], [1, 256]]), in_=vstage[vs][:, 0:256]),
                            reads=["vst%d" % vs])
                    else:
                        vs = t % 2
                        P.op("dve", lambda e, acc=acc, vs=vs: e.tensor_copy(out=vstage[vs][:], in_=acc),
                             reads=["acc%d" % a], writes=["vst%d" % vs])
                        P.dma("sp", "vst%d" % vs, lambda e, t=t, vs=vs, g=g: e.dma_start(
                            out=bass.AP(kvown, VB_OFF + t * 128 * 1024 + (g - 9) * 512, [[1024, 128], [1, 512]]), in_=vstage[vs][:]),
                            reads=["vst%d" % vs])
            else:
                for tg in range(4):
                    fs = (g * 4 + tg) % 2
                    for cc in range(4):
                        a = next_acc()
                        acc = hb_[a]
                        for c in range(16):
                            P.op("pe", lambda e, acc=acc, c=c, cc=cc, tg=tg, s=s: e.matmul(acc, lhsT=wbuf[s][:, c, cc * 128:(cc + 1) * 128], rhs=hT[:, c, tg * 512:(tg + 1) * 512], start=(c == 0), stop=(c == 15)),
                                 reads=["wb%d" % s], writes=["acc%d" % a])
                        if g in (3, 4, 11, 12):
                            P.op("act", lambda e, acc=acc, fs=fs, cc=cc: e.activation(out=fstage[fs][:, cc, :], in_=acc, func=AF.Silu),
                                 reads=["acc%d" % a], writes=["fst%d" % fs])
                        elif cc % 2 == 0:
                            P.op("act", lambda e, acc=acc, fs=fs, cc=cc: e.activation(out=fstage[fs][:, cc, :], in_=acc, func=AF.Copy),
                                 reads=["acc%d" % a], writes=["fst%d" % fs])
                        else:
                            P.op("dve", lambda e, acc=acc, fs=fs, cc=cc: e.tensor_copy(out=fstage[fs][:, cc, :], in_=acc),
                                 reads=["acc%d" % a], writes=["fst%d" % fs])
                    if g in (3, 4):
                        dst = gT.ap()[(g - 3) * 4:(g - 3) * 4 + 4, :, tg * 512:(tg + 1) * 512].rearrange("h d t -> d h t")
                    elif g in (11, 12):
                        dst = gT.ap()[8 + (g - 11) * 4:8 + (g - 11) * 4 + 4, :, tg * 512:(tg + 1) * 512].rearrange("h d t -> d h t")
                    elif g in (5, 6):
                        dst = qbT.ap()[(g - 5) * 4:(g - 5) * 4 + 4, :, tg * 512:(tg + 1) * 512].rearrange("h d t -> d h t")
                    else:
                        r0 = 512 + (g - 7) * 512
                        dst = kvown.ap()[r0:r0 + 512, tg * 512:(tg + 1) * 512].rearrange("(h d) t -> d h t", h=4)
                    P.dma("sp", "fst%d" % fs, lambda e, dst=dst, fs=fs: e.dma_start(out=dst, in_=fstage[fs][:]), reads=["fst%d" % fs])
        P.barrier()

    import os
    STOP = int(os.environ.get("KSTOP", "9"))
    if STOP < 1:
        return
    for ch in range(10):
        P.cc(lambda e, ch=ch: e.collective_compute("AllGather", ALU.bypass, replica_groups=[[0, 1, 2, 3], [4, 5, 6, 7]],
                                                   ins=[kvown.ap()[ch * 256:(ch + 1) * 256, :]],
                                                   outs=[kvbig.ap()[ch * 1536 + 256:ch * 1536 + 1280, :]]))
    P.barrier()

    if STOP < 2:
        return
    with contextlib.ExitStack() as st:
        sb = lambda name, shape, dt: st.enter_context(nc.sbuf_tensor(L + name, shape, dt))
        KT = sb("KT", [128, 2, SEQ], BF16)
        ps, _ = psum_f32(st, "pa", 4)
        V = sb("V", [128, 64, 256], BF16)
        QT = [sb("QT%d" % i, [128, 512], BF16) for i in range(2)]
        PT = [sb("PT%d" % i, [128, 1024], BF16) for i in range(3)]
        rden = sb("rden", [128, 512], F32)
        yst = [sb("yst%d" % i, [128, 512], F32) for i in range(2)]
        P.dma("pool", "kvwin", lambda e: e.dma_start(out=kvwin.ap().rearrange("(c r) n -> c r n", c=8),
                                                     in_=bass.AP(kvbig, jreg["j"] * (256 * 2048) + 2 * 1536 * 2048, [[1536 * 2048, 8], [2048, 768], [1, 2048]])))
        for kv in range(2):
            for r in range(4):
                row0 = (1 + r) * 256 + kv * 128
                P.dma("sp", "KT", lambda e, kv=kv, r=r, row0=row0: e.dma_start(out=KT[:, kv, r * 2048:(r + 1) * 2048], in_=kvbig.ap()[row0:row0 + 128, :]), writes=["KT"])
        for r in range(4):
            P.dma("sp", "V", lambda e, r=r: e.dma_start(out=V[:, r * 16:(r + 1) * 16, :], in_=bass.AP(kvbig, (1536 + (1 + r) * 256) * 2048, [[256, 128], [32768, 16], [1, 256]])), writes=["V"])
        it = 0
        pti = 0
        for h in range(8):
            kv = h // 4
            for qg in range(4):
                qs = it % 2
                osl = it % 2
                it += 1
                O = ps[2][:, osl * 512:(osl + 1) * 512]
                Dn = ps[3][:, osl * 512:(osl + 1) * 512]
                P.dma("sp", "QT%d" % qs, lambda e, h=h, qg=qg, qs=qs: e.dma_start(out=QT[qs][:], in_=qaT.ap()[h, :, qg * 512:(qg + 1) * 512]), writes=["QT%d" % qs])

                def emit_S(kp, qs=qs, kv=kv):
                    ss_ = kp % 2
                    for u in range(2):
                        kt = 2 * kp + u
                        P.op("pe", lambda e, ss_=ss_, u=u, kt=kt: e.matmul(ps[ss_][:, u * 512:(u + 1) * 512], lhsT=KT[:, kv, kt * 128:(kt + 1) * 128], rhs=QT[qs][:], start=True, stop=True),
                             reads=["KT", "QT%d" % qs], writes=["S%d" % ss_])
                emit_S(0)
                for kp in range(32):
                    if kp + 1 < 32:
                        emit_S(kp + 1)
                    ss_ = kp % 2
                    pp = pti % 3
                    pti += 1
                    P.op("act", lambda e, ss_=ss_, pp=pp: e.activation(out=PT[pp][:], in_=ps[ss_][:], func=AF.Exp, scale=SCALE),
                         reads=["S%d" % ss_], writes=["PT%d" % pp])
                    for u in range(2):
                        kt = 2 * kp + u
                        P.op("pe", lambda e, pp=pp, u=u, kt=kt, O=O, kv=kv: e.matmul(O, lhsT=V[:, kt, kv * 128:(kv + 1) * 128], rhs=PT[pp][:, u * 512:(u + 1) * 512], start=(kt == 0), stop=(kt == 63)),
                             reads=["V", "PT%d" % pp], writes=["O%d" % osl])
                        P.op("pe", lambda e, pp=pp, u=u, kt=kt, Dn=Dn: e.matmul(Dn, lhsT=ones[:], rhs=PT[pp][:, u * 512:(u + 1) * 512], start=(kt == 0), stop=(kt == 63)),
                             reads=["ones", "PT%d" % pp], writes=["Dn%d" % osl])
                ys = osl
                P.op("dve", lambda e, Dn=Dn: e.reciprocal(out=rden[:], in_=Dn), reads=["Dn%d" % osl], writes=["rden"])
                P.op("dve", lambda e, O=O, ys=ys: e.tensor_tensor(out=yst[ys][:], in0=O, in1=rden[:], op=ALU.mult), reads=["O%d" % osl, "rden"], writes=["yst%d" % ys])
                P.dma("sp", "yst%d" % ys, lambda e, h=h, qg=qg, ys=ys: e.dma_start(out=yT.ap()[h, :, qg * 512:(qg + 1) * 512], in_=yst[ys][:]), reads=["yst%d" % ys])
        P.barrier()

    if STOP < 3:
        return
    with contextlib.ExitStack() as st:
        sb = lambda name, shape, dt: st.enter_context(nc.sbuf_tensor(L + name, shape, dt))
        KB = [sb("KB%d" % i, [128, 4096], BF16) for i in range(2)]
        ps, hb_ = psum_f32(st, "pb", 4)
        VB = [sb("VB%d" % i, [128, 32, 128], BF16) for i in range(2)]
        QB = [sb("QB%d" % i, [128, TOK], BF16) for i in range(2)]
        MK = sb("MK", [128, 8, 17 * 128], BF16)
        ed = sb("ed", [128, 2], F32)
        E = [sb("E%d" % i, [128, 512], F32) for i in range(2)]
        PB = [sb("PB%d" % i, [128, 512], BF16) for i in range(3)]
        rden = sb("rdenb", [128, 512], F32)
        yst = [sb("ystb%d" % i, [128, 512], F32) for i in range(2)]
        P.dma("sp", "MK", lambda e: e.dma_start(out=MK[:], in_=maskB.ap().rearrange("p (h m) -> p h m", h=8)), writes=["MK"])
        P.dma("sp", "ed", lambda e: e.dma_start(out=ed[:], in_=edge.ap()), writes=["ed"])
        si = 0
        ei = 0
        pbi = 0
        oi = 0
        for h in range(8):
            bs = h % 2
            for (so, tok0, ntok, wc0) in ((0, 1024, 1024, 0), (1, 0, 2048, 1024), (2, 0, 1024, 3072)):
                P.dma("sp", "KB%d" % bs, lambda e, h=h, bs=bs, so=so, tok0=tok0, ntok=ntok, wc0=wc0: e.dma_start(
                    out=KB[bs][:, wc0:wc0 + ntok],
                    in_=bass.AP(kvwin, ((h // 2) * 768 + so * 256 + (h % 2) * 128) * 2048 + tok0, [[2048, 128], [1, ntok]])),
                    writes=["KB%d" % bs])
                for tq in range(tok0 // 512, (tok0 + ntok) // 512):
                    wt0 = (wc0 + tq * 512 - tok0) // 128
                    P.dma("sp", "VB%d" % bs, lambda e, h=h, bs=bs, so=so, tq=tq, wt0=wt0: e.dma_start(
                        out=VB[bs][:, wt0:wt0 + 4, :],
                        in_=bass.AP(kvwin, ((4 + tq) * 768 + so * 256) * 2048 + h * 128, [[1024, 128], [131072, 4], [1, 128]])),
                        writes=["VB%d" % bs])
            P.dma("sp", "QB%d" % bs, lambda e, h=h, bs=bs: e.dma_start(out=QB[bs][:], in_=qbT.ap()[h, :, :]), writes=["QB%d" % bs])
            for qt in range(NT):
                qsl = qt % 4
                if qsl == 0:
                    osl = oi % 2
                    oi += 1
                O = ps[2][:, osl * 512 + qsl * 128: osl * 512 + (qsl + 1) * 128]
                Dn = ps[3][:, osl * 512 + qsl * 128: osl * 512 + (qsl + 1) * 128]
                groups = [list(range(qt + 4 * gi, min(qt + 4 * gi + 4, qt + 17))) for gi in range(5)]
                for grp in groups:
                    n = len(grp)
                    sk = si % 4
                    si += 1
                    S = hb_[sk]
                    for i, wt in enumerate(grp):
                        P.op("pe", lambda e, S=S, i=i, wt=wt, bs=bs, qt=qt: e.matmul(S[:, i * 128:(i + 1) * 128], lhsT=KB[bs][:, wt * 128:(wt + 1) * 128], rhs=QB[bs][:, qt * 128:(qt + 1) * 128], start=True, stop=True),
                             reads=["KB%d" % bs, "QB%d" % bs], writes=["SB%d" % sk])
                    es = ei % 2
                    ei += 1
                    P.op("act", lambda e, S=S, es=es, n=n: e.activation(out=E[es][:, 0:n * 128], in_=S[:, 0:n * 128], func=AF.Exp, scale=SCALE),
                         reads=["SB%d" % sk], writes=["E%d" % es])
                    pb = pbi % 3
                    pbi += 1
                    cls = ["L" if wt < 8 else ("R" if wt >= 24 else "O") for wt in grp]
                    i0 = 0
                    while i0 < n:
                        i1 = i0
                        while i1 < n and cls[i1] == cls[i0]:
                            i1 += 1
                        jj0 = grp[i0] - qt
                        msl = MK[:, h, jj0 * 128:(jj0 + (i1 - i0)) * 128]
                        if cls[i0] == "O":
                            P.op("dve", lambda e, pb=pb, es=es, i0=i0, i1=i1, msl=msl: e.tensor_tensor(out=PB[pb][:, i0 * 128:i1 * 128], in0=E[es][:, i0 * 128:i1 * 128], in1=msl, op=ALU.mult),
                                 reads=["E%d" % es, "MK"], writes=["PB%d" % pb])
                        else:
                            col = 0 if cls[i0] == "L" else 1
                            P.op("dve", lambda e, pb=pb, es=es, i0=i0, i1=i1, msl=msl, col=col: e.scalar_tensor_tensor(out=PB[pb][:, i0 * 128:i1 * 128], in0=E[es][:, i0 * 128:i1 * 128], scalar=ed[:, col:col + 1], in1=msl, op0=ALU.mult, op1=ALU.mult),
                                 reads=["E%d" % es, "MK", "ed"], writes=["PB%d" % pb])
                        i0 = i1
                    for i, wt in enumerate(grp):
                        first = (wt == qt)
                        lastk = (wt == qt + 16)
                        P.op("pe", lambda e, O=O, pb=pb, i=i, wt=wt, bs=bs, first=first, lastk=lastk: e.matmul(O, lhsT=VB[bs][:, wt, :], rhs=PB[pb][:, i * 128:(i + 1) * 128], start=first, stop=lastk),
                             reads=["VB%d" % bs, "PB%d" % pb], writes=["OB%d" % osl])
                        P.op("pe", lambda e, Dn=Dn, pb=pb, i=i, first=first, lastk=lastk: e.matmul(Dn, lhsT=ones[:], rhs=PB[pb][:, i * 128:(i + 1) * 128], start=first, stop=lastk),
                             reads=["ones", "PB%d" % pb], writes=["DB%d" % osl])
                if qsl == 3:
                    Of = ps[2][:, osl * 512:(osl + 1) * 512]
                    Df = ps[3][:, osl * 512:(osl + 1) * 512]
                    ys = osl
                    qg = qt // 4
                    P.op("dve", lambda e, Df=Df: e.reciprocal(out=rden[:], in_=Df), reads=["DB%d" % osl], writes=["rdenb"])
                    P.op("dve", lambda e, Of=Of, ys=ys: e.tensor_tensor(out=yst[ys][:], in0=Of, in1=rden[:], op=ALU.mult), reads=["OB%d" % osl, "rdenb"], writes=["ystb%d" % ys])
                    P.dma("sp", "ystb%d" % ys, lambda e, h=h, qg=qg, ys=ys: e.dma_start(out=yT.ap()[8 + h, :, qg * 512:(qg + 1) * 512], in_=yst[ys][:]), reads=["ystb%d" % ys])
        P.barrier()

    if STOP < 4:
        return
    with contextlib.ExitStack() as st:
        sb = lambda name, shape, dt: st.enter_context(nc.sbuf_tensor(L + name, shape, dt))
        WO = sb("WO", [128, 16, D], BF16)
        ps, hb_ = psum_f32(st, "po", 3)
        yt = [sb("yt%d" % i, [128, 16, 256], F32) for i in range(2)]
        gt = [sb("gt%d" % i, [128, 16, 256], BF16) for i in range(2)]
        zt = [sb("zt%d" % i, [128, 16, 256], BF16) for i in range(2)]
        sqo = [sb("sqo%d" % i, [128, 256], F32) for i in range(2)]
        tmp = [sb("tmp%d" % i, [128, 256], F32) for i in range(2)]
        rstd = [sb("rstd%d" % i, [128, 256], F32) for i in range(2)]
        onw = sb("onw", [128, 16], F32)
        xo = [sb("xo%d" % i, [128, D], F32) for i in range(2)]
        fw = sb("fw", [128, D], F32)
        sqf = sb("sqf", [128, D], BF16)
        ssf = sb("ssf", [128, 2], F32)
        for half in range(2):
            P.dma("pool", "WO", lambda e, half=half: e.dma_start(out=WO[:, half * 8:(half + 1) * 8, :], in_=w_out.ap()[li, half * 1024:(half + 1) * 1024, :].rearrange("(c p) n -> p c n", p=128)), writes=["WO"])
        P.dma("sp", "onw", lambda e: e.dma_start(out=onw[:, 0:8], in_=bass.AP(ona, li * 1024, [[1, 128], [128, 8]]), allow_slow_non_contiguous=True), writes=["onw"])
        P.dma("sp", "onw", lambda e: e.dma_start(out=onw[:, 8:16], in_=bass.AP(onb, li * 1024, [[1, 128], [128, 8]]), allow_slow_non_contiguous=True), writes=["onw"])
        if do_final:
            P.dma("sp", "fw", lambda e: e.dma_start(out=fw[:], in_=bcast_row(fnw, 0, D)), writes=["fw"])
        oacc = 0
        for tg in range(8):
            s = tg % 2
            c0 = tg * 256
            P.dma("sp", "yt%d" % s, lambda e, s=s, c0=c0: e.dma_start(out=yt[s][:], in_=yT.ap()[:, :, c0:c0 + 256].rearrange("c d t -> d c t")), writes=["yt%d" % s])
            P.dma("sp", "gt%d" % s, lambda e, s=s, c0=c0: e.dma_start(out=gt[s][:], in_=gT.ap()[:, :, c0:c0 + 256].rearrange("c d t -> d c t")), writes=["gt%d" % s])
            for mix in range(2):
                ssp = hb_[4 + mix]
                for c in range(8):
                    ch = mix * 8 + c
                    q = ch % 2
                    P.op("act", lambda e, s=s, ch=ch, q=q: e.activation(out=sqo[q][:], in_=yt[s][:, ch, :], func=AF.Square),
                         reads=["yt%d" % s], writes=["sqo%d" % q])
                    P.op("pe", lambda e, ssp=ssp, q=q, c=c: e.matmul(ssp[:, 0:256], lhsT=onesf[:], rhs=sqo[q][:], start=(c == 0), stop=(c == 7)),
                         reads=["onesf", "sqo%d" % q], writes=["ssp%d" % mix])
                P.op("act", lambda e, ssp=ssp, mix=mix: e.activation(out=rstd[mix][:], in_=ssp[:, 0:256], func=AF.Sqrt, scale=1.0 / 1024, bias=epsb[:]),
                     reads=["ssp%d" % mix, "epsb"], writes=["rstd%d" % mix])
                P.op("dve", lambda e, mix=mix: e.reciprocal(out=rstd[mix][:], in_=rstd[mix][:]),
                     reads=["rstd%d" % mix], writes=["rstd%d" % mix])
                for c in range(8):
                    ch = mix * 8 + c
                    q = ch % 2
                    P.op("dve", lambda e, s=s, ch=ch, q=q, mix=mix: e.tensor_tensor(out=tmp[q][:], in0=yt[s][:, ch, :], in1=rstd[mix][:], op=ALU.mult),
                         reads=["yt%d" % s, "rstd%d" % mix], writes=["tmp%d" % q])
                    P.op("dve", lambda e, s=s, ch=ch, q=q: e.scalar_tensor_tensor(out=zt[s][:, ch, :], in0=gt[s][:, ch, :], scalar=onw[:, ch:ch + 1], in1=tmp[q][:], op0=ALU.mult, op1=ALU.mult),
                         reads=["gt%d" % s, "onw", "tmp%d" % q], writes=["zt%d" % s])
            for tt in range(2):
                t = tg * 2 + tt
                xs = t % 2
                P.dma("sp", "xo%d" % xs, lambda e, t=t, xs=xs: e.dma_start(out=xo[xs][:], in_=x_src.ap()[t * 128:(t + 1) * 128, :]), writes=["xo%d" % xs])
                for cg in range(4):
                    a = oacc % 4
                    oacc += 1
                    acc = hb_[a]
                    for c in range(16):
                        P.op("pe", lambda e, acc=acc, c=c, s=s, tt=tt, cg=cg: e.matmul(acc, lhsT=zt[s][:, c, tt * 128:(tt + 1) * 128], rhs=WO[:, c, cg * 512:(cg + 1) * 512], start=(c == 0), stop=(c == 15)),
                             reads=["zt%d" % s, "WO"], writes=["oacc%d" % a])
                    P.op("dve", lambda e, acc=acc, xs=xs, cg=cg: e.tensor_tensor(out=xo[xs][:, cg * 512:(cg + 1) * 512], in0=acc, in1=xo[xs][:, cg * 512:(cg + 1) * 512], op=ALU.add),
                         reads=["oacc%d" % a, "xo%d" % xs], writes=["xo%d" % xs])
                if do_final:
                    P.op("act", lambda e, xs=xs: e.activation(out=sqf[:], in_=xo[xs][:], func=AF.Square, accum_out=ssf[:, xs:xs + 1]),
                         reads=["xo%d" % xs], writes=["sqf", "ssf%d" % xs])
                    P.op("act", lambda e, xs=xs: e.activation(out=ssf[:, xs:xs + 1], in_=ssf[:, xs:xs + 1], func=AF.Sqrt, scale=1.0 / D, bias=epsb[:]),
                         reads=["ssf%d" % xs, "epsb"], writes=["ssf%d" % xs])
                    P.op("dve", lambda e, xs=xs: e.reciprocal(out=ssf[:, xs:xs + 1], in_=ssf[:, xs:xs + 1]),
                         reads=["ssf%d" % xs], writes=["ssf%d" % xs])
                    P.op("dve", lambda e, xs=xs: e.scalar_tensor_tensor(out=xo[xs][:], in0=xo[xs][:], scalar=ssf[:, xs:xs + 1], in1=fw[:], op0=ALU.mult, op1=ALU.mult),
                         reads=["xo%d" % xs, "ssf%d" % xs, "fw"], writes=["xo%d" % xs])
                P.dma("sp", "xo%d" % xs, lambda e, t=t, xs=xs: e.dma_start(out=x_dst.ap()[t * 128:(t + 1) * 128, :], in_=xo[xs][:]), reads=["xo%d" % xs])
        P.barrier()


def _rope_tables():
    t = np.arange(SEQ)
    row = (t // 64).astype(np.float32)
    col = (t % 64).astype(np.float32)
    inv = (10000.0 ** (-np.arange(0, 64, 2, dtype=np.float32) / 64)).astype(np.float32)
    ar = row[:, None] * inv[None, :]
    ac = col[:, None] * inv[None, :]
    cr, sr, cc, sc = np.cos(ar), np.sin(ar), np.cos(ac), np.sin(ac)
    C = np.concatenate([cr, cr, cc, cc], axis=1).astype(np.float32)
    S = np.concatenate([-sr, sr, -sc, sc], axis=1).astype(np.float32)
    return C, S


def _mask_b():
    slopes = (2.0 ** -np.arange(1, 9)).astype(np.float64)
    k = np.arange(128)[:, None]
    q = np.arange(128)[None, :]
    M = np.zeros((128, 8, 17, 128), np.float64)
    for jj in range(17):
        off = (jj - 8) * 128 + k - q
        a = np.abs(off)
        mult = np.zeros_like(off, dtype=np.float64)
        for w, d in PATTERNS:
            mult += ((a <= w // 2) & (off % d == 0)).astype(np.float64)
        for h in range(8):
            M[:, h, jj, :] = mult * np.exp(-slopes[h] * a)
    return M.reshape(128, 8 * 17 * 128).astype(ml_dtypes.bfloat16)


_CACHE = {}


def _get_prog(nl, first, last):
    key = (nl, last)
    if key not in _CACHE:
        _CACHE[key] = build_program(nl, first, last)
    return _CACHE[key]


def _consts():
    if "c" not in _CACHE:
        C, S = _rope_tables()
        _CACHE["c"] = (C, S, _mask_b(), np.eye(128, dtype=np.float32).astype(ml_dtypes.bfloat16))
    return _CACHE["c"]


def run_layers(xs, layers, first, last, w):
    nl = len(layers)
    nc = _get_prog(nl, first, last)
    C, S, M, I = _consts()
    f = lambda a: np.ascontiguousarray(a, dtype=np.float32)
    sel = lambda a: f(a[layers])
    in_maps = []
    for c in range(8):
        j = c % 4
        edge = np.zeros((128, 2), np.float32)
        edge[:, 0] = 1.0 if j > 0 else 0.0
        edge[:, 1] = 1.0 if j < 3 else 0.0
        in_maps.append({
            "x": f(xs[c]), "w_in": sel(w["w_in"]), "w_out": sel(w["w_out"]), "norm_w": sel(w["norm_w"]),
            "q_norm_a": sel(w["q_norm_a"]), "k_norm_a": sel(w["k_norm_a"]),
            "out_norm_a": sel(w["out_norm_a"]), "out_norm_b": sel(w["out_norm_b"]),
            "final_norm": f(w["final_norm"]).reshape(1, D),
            "ropeC": f(C[j * TOK:(j + 1) * TOK]), "ropeS": f(S[j * TOK:(j + 1) * TOK]),
            "maskB": M, "edge": edge, "ident": I,
        })
    res = run_bass_kernel_spmd(nc, in_maps, core_ids=list(range(8)))
    return [r["out"] for r in res.results]


LAUNCH_GROUPS = [[0, 1, 2, 3]]


def kernel(x, norm_w, w_in, q_norm_a, k_norm_a, out_norm_a, out_norm_b, w_out, final_norm):
    w = dict(norm_w=np.asarray(norm_w), w_in=np.asarray(w_in), q_norm_a=np.asarray(q_norm_a), k_norm_a=np.asarray(k_norm_a),
             out_norm_a=np.asarray(out_norm_a), out_norm_b=np.asarray(out_norm_b), w_out=np.asarray(w_out), final_norm=np.asarray(final_norm))
    x = np.asarray(x)
    xs = [x[c // 4, (c % 4) * TOK:(c % 4 + 1) * TOK, :] for c in range(8)]
    ng = len(LAUNCH_GROUPS)
    for gi, grp in enumerate(LAUNCH_GROUPS):
        xs = run_layers(xs, grp, gi == 0, gi == ng - 1, w)
    out = np.empty((2, SEQ, D), np.float32)
    for c in range(8):
        out[c // 4, (c % 4) * TOK:(c % 4 + 1) * TOK, :] = xs[c]
    return out
```

```python
import contextlib
import numpy as np
import ml_dtypes
import concourse.bass as bass
import concourse.mybir as mybir
from concourse.bass_utils import run_bass_kernel_spmd

F32 = mybir.dt.float32
BF16 = mybir.dt.bfloat16
ALU = mybir.AluOpType
AF = mybir.ActivationFunctionType
AX = mybir.AxisListType

ENGS = ["pe", "act", "dve", "pool", "sp"]

D = 2048
SEQ = 8192
TOK = 2048
NT = TOK // 128
DIN = 6656
HD = 128
EPS = 1e-6
SCALE = HD ** -0.5
RANK = 2560 * 2048
KA_OFF = 0
VA_OFF = 256 * 2048
KB_OFF = 512 * 2048
VB_OFF = 1536 * 2048
PATTERNS = ((128, 1), (512, 4), (2048, 16))


class Prog:
    def __init__(self, nc, n_dma_sems=110):
        self.nc = nc
        self.segments = []
        self.cur = {e: [] for e in ENGS}
        self.cnt = {e: 0 for e in ENGS}
        self.waited = {e: {} for e in ENGS}
        self.bufs = {}
        self.dtot = {}
        self.n_dma_sems = n_dma_sems
        self.dma_keys = {}
        self.nbar = 0
        self.same_eng_wait = {"pe": False, "act": True, "dve": True, "pool": True, "sp": False}

    def _deps(self, eng, reads, writes):
        evs = []
        for k in reads:
            b = self.bufs.get(k)
            if b and b[0] is not None:
                evs.append(b[0])
        for k in writes:
            b = self.bufs.get(k)
            if b:
                if b[0] is not None:
                    evs.append(b[0])
                evs.extend(b[1])
        need = {}
        for (sk, v) in evs:
            if sk == eng and not self.same_eng_wait[eng]:
                continue
            if self.waited[eng].get(sk, 0) >= v:
                continue
            if need.get(sk, 0) < v:
                need[sk] = v
        for sk, v in need.items():
            self.waited[eng][sk] = v
        return list(need.items())

    def _record(self, ev, reads, writes):
        for k in reads:
            self.bufs.setdefault(k, [None, []])[1].append(ev)
        for k in writes:
            self.bufs[k] = [ev, []]

    def op(self, eng, fn, reads=(), writes=()):
        waits = self._deps(eng, reads, writes)
        self.cnt[eng] += 1
        ev = (eng, self.cnt[eng])
        self.cur[eng].append(("op", fn, waits, ev))
        self._record(ev, reads, writes)
        return ev

    def dma(self, eng, semkey, fn, reads=(), writes=()):
        if semkey not in self.dma_keys:
            assert len(self.dma_keys) < self.n_dma_sems, "out of dma sems"
            self.dma_keys[semkey] = len(self.dma_keys)
        waits = self._deps(eng, reads, writes)
        self.dtot[semkey] = self.dtot.get(semkey, 0) + 16
        ev = (semkey, self.dtot[semkey])
        self.cur[eng].append(("dma", fn, waits, ev))
        self._record(ev, reads, writes)
        return ev

    def cc(self, fn, writes=()):
        semkey = "_cc"
        if semkey not in self.dma_keys:
            self.dma_keys[semkey] = len(self.dma_keys)
        self.dtot[semkey] = self.dtot.get(semkey, 0) + 1
        ev = (semkey, self.dtot[semkey])
        self.cur["pool"].append(("cc", fn, [], ev))
        self._record(ev, (), writes)

    def barrier(self):
        self.nbar += 1
        self.cur["_bar"] = dict(cnt=dict(self.cnt), dtot=dict(self.dtot), idx=self.nbar)
        self.segments.append(self.cur)
        self.cur = {e: [] for e in ENGS}
        self.bufs = {}

    def emit(self):
        nc = self.nc
        if any(self.cur[e] for e in ENGS):
            self.barrier()
        segs = self.segments
        needed = {e: set() for e in ENGS}
        for seg in segs:
            for en in ENGS:
                for (kind, fn, waits, ev) in seg[en]:
                    for (sk, v) in waits:
                        if sk in needed:
                            needed[sk].add(v)
                if seg["_bar"]["cnt"][en] > 0:
                    needed[en].add(seg["_bar"]["cnt"][en])
        rank = {e: {v: i + 1 for i, v in enumerate(sorted(needed[e]))} for e in ENGS}

        def wval(sk, v):
            return rank[sk][v] if sk in rank else v
        with contextlib.ExitStack() as st:
            esem = {e: st.enter_context(nc.semaphore("s_" + e)) for e in ENGS}
            dsem = [st.enter_context(nc.semaphore("d%d" % i)) for i in range(len(self.dma_keys))]
            b1 = st.enter_context(nc.semaphore("bar1"))
            b2 = st.enter_context(nc.semaphore("bar2"))
            block = st.enter_context(nc.Block())

            def sem_of(sk):
                if sk in esem:
                    return esem[sk]
                return dsem[self.dma_keys[sk]]

            def run(eng_name, e):
                for seg in segs:
                    for (kind, fn, waits, ev) in seg[eng_name]:
                        for (sk, v) in waits:
                            e.wait_ge(sem_of(sk), wval(sk, v))
                        ins = fn(e)
                        if kind == "op":
                            if ev[1] in needed[eng_name]:
                                ins.then_inc(esem[eng_name], 1)
                        elif kind == "dma":
                            ins.then_inc(sem_of(ev[0]), 16)
                        elif kind == "cc":
                            ins.then_inc(sem_of(ev[0]), 1)
                    bar = seg["_bar"]
                    if bar["cnt"][eng_name] > 0:
                        e.wait_ge(esem[eng_name], wval(eng_name, bar["cnt"][eng_name]))
                    if eng_name != "pool":
                        e.sem_inc(b1, 1)
                        e.wait_ge(b2, bar["idx"])
                    else:
                        e.wait_ge(b1, (len(ENGS) - 1) * bar["idx"])
                        for sk, v in bar["dtot"].items():
                            e.wait_ge(sem_of(sk), v)
                        e.sem_inc(b2, 1)

            @block.tensor
            def _(e):
                run("pe", e)

            @block.scalar
            def _(e):
                run("act", e)

            @block.vector
            def _(e):
                run("dve", e)

            @block.gpsimd
            def _(e):
                self.pool_eng_setup(e)
                run("pool", e)

            @block.sync
            def _(e):
                run("sp", e)

    def pool_eng_setup(self, e):
        pass


def build_program(nl, first, last):
    nc = bass.Bass("TRN2", target_bir_lowering=False)
    dt_in = lambda name, shape, dt=F32: nc.dram_tensor(name, shape, dt, kind="ExternalInput")
    x_in = dt_in("x", [TOK, D])
    w_in = dt_in("w_in", [nl, D, DIN])
    w_out = dt_in("w_out", [nl, D, D])
    norm_w = dt_in("norm_w", [nl, D])
    qn = dt_in("q_norm_a", [nl, HD])
    kn = dt_in("k_norm_a", [nl, HD])
    ona = dt_in("out_norm_a", [nl, 1024])
    onb = dt_in("out_norm_b", [nl, 1024])
    fnw = dt_in("final_norm", [1, D])
    ropeC = dt_in("ropeC", [TOK, HD])
    ropeS = dt_in("ropeS", [TOK, HD])
    maskB = dt_in("maskB", [128, 8 * 17 * 128], BF16)
    edge = dt_in("edge", [128, 2])
    ident_d = dt_in("ident", [128, 128], BF16)
    out_d = nc.dram_tensor("out", [TOK, D], F32, kind="ExternalOutput")

    xres = [nc.dram_tensor("xres%d" % i, [TOK, D], F32) for i in range(2)]
    qaT = nc.dram_tensor("qaT", [8, 128, TOK], BF16)
    qbT = nc.dram_tensor("qbT", [8, 128, TOK], BF16)
    gT = nc.dram_tensor("gT", [16, 128, TOK], BF16)
    yT = nc.dram_tensor("yT", [16, 128, TOK], F32)
    kvown = nc.dram_tensor("kvown", [2560, 2048], BF16)
    kvbig = nc.dram_tensor("kvbig", [10 * 1536, 2048], BF16)
    kvwin = nc.dram_tensor("kvwin", [8 * 768, 2048], BF16)

    P = Prog(nc)
    jreg = {}

    def setup_pool(e):
        pid = e.partition_id()
        jreg["j"] = pid % 4
    P.pool_eng_setup = setup_pool

    with contextlib.ExitStack() as gst:
        ident = gst.enter_context(nc.sbuf_tensor("ident_sb", [128, 128], BF16))
        ones = gst.enter_context(nc.sbuf_tensor("ones", [128, 128], BF16))
        onesf = gst.enter_context(nc.sbuf_tensor("onesf", [128, 128], F32))
        zer = gst.enter_context(nc.sbuf_tensor("zer", [128, 2048], BF16))
        P.dma("sp", "ident", lambda e: e.dma_start(out=ident[:], in_=ident_d.ap()), writes=["ident"])
        epsb = gst.enter_context(nc.sbuf_tensor("epsb", [128, 1], F32))
        P.op("dve", lambda e: e.memset(epsb[:], EPS), writes=["epsb"])
        P.op("dve", lambda e: e.memset(ones[:], 1.0), writes=["ones"])
        P.op("dve", lambda e: e.memset(onesf[:], 1.0), writes=["onesf"])
        P.op("dve", lambda e: e.memset(zer[:], 0.0), writes=["zer"])
        for ch in range(10):
            for slot in (0, 5):
                for part in range(2):
                    r0 = ch * 1536 + slot * 256 + part * 128
                    P.dma("sp", "zpad", lambda e, r0=r0: e.dma_start(out=kvbig.ap()[r0:r0 + 128, :], in_=zer[:]), reads=["zer"])
        P.barrier()

        for li in range(nl):
            x_src = x_in if (li == 0 and first) else xres[(li - 1) % 2]
            if li == 0 and not first:
                x_src = x_in
            is_last = (li == nl - 1)
            x_dst = out_d if is_last else xres[li % 2]
            layer(P, nc, li, x_src, x_dst, is_last and last, locals())
        P.emit()
    return nc


def layer(P, nc, li, x_src, x_dst, do_final, G):
    def psum_f32(stk, name, n):
        tl = [stk.enter_context(nc.psum_tensor(L + name + str(i), [128, 1024], F32)) for i in range(n)]
        halves = [tl[i // 2][:, (i % 2) * 512:(i % 2 + 1) * 512] for i in range(2 * n)]
        return tl, halves
    ident, ones, onesf, epsb = G["ident"], G["ones"], G["onesf"], G["epsb"]
    w_in, w_out, norm_w, qn, kn, ona, onb, fnw = (G[k] for k in ["w_in", "w_out", "norm_w", "qn", "kn", "ona", "onb", "fnw"])
    ropeC, ropeS, maskB, edge = G["ropeC"], G["ropeS"], G["maskB"], G["edge"]
    qaT, qbT, gT, yT, kvown, kvbig = G["qaT"], G["qbT"], G["gT"], G["yT"], G["kvown"], G["kvbig"]
    kvwin = G["kvwin"]
    jreg = G["jreg"]
    L = "L%d" % li

    def bcast_row(handle, row, n):
        return bass.AP(handle, row * n, [[0, 128], [1, n]])

    with contextlib.ExitStack() as st:
        sb = lambda name, shape, dt: st.enter_context(nc.sbuf_tensor(L + name, shape, dt))
        hT = sb("hT", [128, 16, TOK], BF16)
        with contextlib.ExitStack() as st1:
            sb1 = lambda name, shape, dt: st1.enter_context(nc.sbuf_tensor(L + name, shape, dt))
            xt = [sb1("xt%d" % i, [128, D], F32) for i in range(2)]
            hbt = [sb1("hbt%d" % i, [128, D], BF16) for i in range(2)]
            sq = sb1("sq", [128, D], F32)
            ss = sb1("ss", [128, 16], F32)
            rs = sb1("rs", [128, 16], F32)
            nw = sb1("nw", [128, D], F32)
            tpp = [st1.enter_context(nc.psum_tensor(L + "tpp%d" % i, [128, 1024], BF16)) for i in range(2)]
            P.dma("sp", "nw", lambda e: e.dma_start(out=nw[:], in_=bcast_row(norm_w, li, D)), writes=["nw"])
            for t in range(NT):
                s = t % 2
                P.dma("sp", "xt%d" % s, lambda e, t=t, s=s: e.dma_start(out=xt[s][:], in_=x_src.ap()[t * 128:(t + 1) * 128, :]), writes=["xt%d" % s])
                P.op("act", lambda e, t=t, s=s: e.activation(out=sq[:], in_=xt[s][:], func=AF.Square, accum_out=ss[:, t:t + 1]),
                     reads=["xt%d" % s], writes=["sq", "ss%d" % t])
                P.op("act", lambda e, t=t: e.activation(out=rs[:, t:t + 1], in_=ss[:, t:t + 1], func=AF.Sqrt, scale=1.0 / D, bias=epsb[:]),
                     reads=["ss%d" % t, "epsb"], writes=["rs%d" % t])
                P.op("dve", lambda e, t=t: e.reciprocal(out=rs[:, t:t + 1], in_=rs[:, t:t + 1]),
                     reads=["rs%d" % t], writes=["rs%d" % t])
                P.op("dve", lambda e, t=t, s=s: e.scalar_tensor_tensor(out=hbt[s][:], in0=xt[s][:], scalar=rs[:, t:t + 1], in1=nw[:], op0=ALU.mult, op1=ALU.mult),
                     reads=["xt%d" % s, "rs%d" % t, "nw"], writes=["hbt%d" % s])
                for half in range(2):
                    pt = tpp[half][:]
                    for c in range(8):
                        P.op("pe", lambda e, s=s, half=half, c=c, pt=pt: e.transpose(out=pt[:, c * 128:(c + 1) * 128], in_=hbt[s][:, (half * 8 + c) * 128:(half * 8 + c + 1) * 128], identity=ident[:]),
                             reads=["hbt%d" % s, "ident"], writes=["tp%d" % half])
                    eng = "act" if half == 0 else "dve"
                    if eng == "act":
                        P.op("act", lambda e, t=t, half=half, pt=pt: e.activation(out=hT[:, half * 8:(half + 1) * 8, t * 128:(t + 1) * 128], in_=pt.rearrange("p (c k) -> p c k", k=128), func=AF.Copy),
                             reads=["tp%d" % half], writes=["hT"])
                    else:
                        P.op("dve", lambda e, t=t, half=half, pt=pt: e.tensor_copy(out=hT[:, half * 8:(half + 1) * 8, t * 128:(t + 1) * 128], in_=pt.rearrange("p (c k) -> p c k", k=128)),
                             reads=["tp%d" % half], writes=["hT"])
            P.barrier()

        wbuf = [sb("wb%d" % i, [128, 16, 512], BF16) for i in range(2)]
        _, hb_ = psum_f32(st, "acc", 2)
        tpq = st.enter_context(nc.psum_tensor(L + "tpq", [128, 512], BF16))
        Ct = sb("Ct", [128, NT, HD], F32)
        St = sb("St", [128, NT, HD], F32)
        gq = sb("gq", [128, HD], F32)
        gk = sb("gk", [128, HD], F32)
        junk = sb("junk", [128, HD], F32)
        ssq = sb("ssq", [128, 4], F32)
        rsq = sb("rsq", [128, 4], F32)
        xn = [sb("xn%d" % i, [128, HD], F32) for i in range(2)]
        t1 = [sb("t1%d" % i, [128, HD], F32) for i in range(2)]
        t2 = [sb("t2%d" % i, [128, HD], F32) for i in range(2)]
        ro = [sb("ro%d" % i, [128, HD], BF16) for i in range(2)]
        qstage = [sb("qst%d" % i, [128, 4, 512], BF16) for i in range(2)]
        fstage = [sb("fst%d" % i, [128, 4, 512], BF16) for i in range(2)]
        vstage = [sb("vst%d" % i, [128, 512], BF16) for i in range(2)]
        P.dma("sp", "Ct", lambda e: e.dma_start(out=Ct[:], in_=ropeC.ap().rearrange("(t p) d -> p t d", p=128)), writes=["Ct"])
        P.dma("sp", "St", lambda e: e.dma_start(out=St[:], in_=ropeS.ap().rearrange("(t p) d -> p t d", p=128)), writes=["St"])
        P.dma("sp", "gq", lambda e: e.dma_start(out=gq[:], in_=bcast_row(qn, li, HD)), writes=["gq"])
        P.dma("sp", "gk", lambda e: e.dma_start(out=gk[:], in_=bcast_row(kn, li, HD)), writes=["gk"])

        T_GROUPS = {0, 1, 2, 9, 10}
        acc_i = [0]
        tp_i = [0]
        hd_i = [0]

        def next_acc():
            a = acc_i[0] % 4
            acc_i[0] += 1
            return a

        def rope_head(acc_ap, accs_key, t, gain, out_stage, out_key, hslot, tcol):
            i = hd_i[0] % 2
            hd_i[0] += 1
            k = "h%d" % i
            P.op("act", lambda e: e.activation(out=junk[:], in_=acc_ap, func=AF.Square, accum_out=ssq[:, i:i + 1]),
                 reads=[accs_key], writes=["junk", "ssq" + k])
            P.op("act", lambda e: e.activation(out=rsq[:, i:i + 1], in_=ssq[:, i:i + 1], func=AF.Sqrt, scale=1.0 / HD, bias=epsb[:]),
                 reads=["ssq" + k, "epsb"], writes=["rsq" + k])
            P.op("dve", lambda e: e.reciprocal(out=rsq[:, i:i + 1], in_=rsq[:, i:i + 1]),
                 reads=["rsq" + k], writes=["rsq" + k])
            P.op("dve", lambda e: e.scalar_tensor_tensor(out=xn[i][:], in0=acc_ap, scalar=rsq[:, i:i + 1], in1=gain[:], op0=ALU.mult, op1=ALU.mult),
                 reads=[accs_key, "rsq" + k, "gq", "gk"], writes=["xn" + k])
            P.op("dve", lambda e: e.tensor_tensor(out=t1[i][:], in0=xn[i][:], in1=Ct[:, t, :], op=ALU.mult),
                 reads=["xn" + k, "Ct"], writes=["t1" + k])
            xv = xn[i][:].rearrange("p (a h c) -> p a h c", a=2, h=2)
            sv = St[:, t, :].rearrange("p (a h c) -> p a h c", a=2, h=2)
            tv = t2[i][:].rearrange("p (a h c) -> p a h c", a=2, h=2)
            P.op("dve", lambda e: e.tensor_tensor(out=tv[:, :, 0, :], in0=xv[:, :, 1, :], in1=sv[:, :, 0, :], op=ALU.mult),
                 reads=["xn" + k, "St"], writes=["t2a" + k])
            P.op("dve", lambda e: e.tensor_tensor(out=tv[:, :, 1, :], in0=xv[:, :, 0, :], in1=sv[:, :, 1, :], op=ALU.mult),
                 reads=["xn" + k, "St"], writes=["t2b" + k])
            P.op("dve", lambda e: e.tensor_tensor(out=ro[i][:], in0=t1[i][:], in1=t2[i][:], op=ALU.add),
                 reads=["t1" + k, "t2a" + k, "t2b" + k], writes=["ro" + k])
            tpi = tp_i[0] % 4
            tp_i[0] += 1
            pt = tpq[:, tpi * 128:(tpi + 1) * 128]
            P.op("pe", lambda e: e.transpose(out=pt, in_=ro[i][:], identity=ident[:]),
                 reads=["ro" + k, "ident"], writes=["tpq%d" % tpi])
            P.op("act", lambda e: e.activation(out=out_stage[:, hslot, tcol * 128:(tcol + 1) * 128], in_=pt, func=AF.Copy),
                 reads=["tpq%d" % tpi], writes=[out_key])

        for g in range(13):
            s = g % 2
            P.dma("pool", "wb%d" % s, lambda e, g=g, s=s: e.dma_start(out=wbuf[s][:], in_=w_in.ap()[li, :, g * 512:(g + 1) * 512].rearrange("(c p) n -> p c n", p=128)),
                  writes=["wb%d" % s])
            if g in T_GROUPS:
                for t in range(NT):
                    a = next_acc()
                    acc = hb_[a]
                    for c in range(16):
                        P.op("pe", lambda e, acc=acc, c=c, t=t, s=s: e.matmul(acc, lhsT=hT[:, c, t * 128:(t + 1) * 128], rhs=wbuf[s][:, c, :], start=(c == 0), stop=(c == 15)),
                             reads=["wb%d" % s], writes=["acc%d" % a])
                    tg, tc = t // 4, t % 4
                    if g in (0, 1):
                        qs = (g * 4 + tg) % 2
                        for hh in range(4):
                            rope_head(acc[:, hh * 128:(hh + 1) * 128], "acc%d" % a, t, gq, qstage[qs], "qst%d" % qs, hh, tc)
                        if tc == 3:
                            P.dma("sp", "qst%d" % qs, lambda e, g=g, tg=tg, qs=qs: e.dma_start(
                                out=qaT.ap()[g * 4:(g + 1) * 4, :, tg * 512:(tg + 1) * 512].rearrange("h d t -> d h t"), in_=qstage[qs][:]),
                                reads=["qst%d" % qs])
                    elif g == 2:
                        qs = tg % 2
                        for hh in range(2):
                            rope_head(acc[:, hh * 128:(hh + 1) * 128], "acc%d" % a, t, gk, qstage[qs], "qst%d" % qs, hh, tc)
                        if tc == 3:
                            P.dma("sp", "qst%d" % qs, lambda e, tg=tg, qs=qs: e.dma_start(
                                out=kvown.ap()[0:256, tg * 512:(tg + 1) * 512].rearrange("(h d) t -> d h t", h=2), in_=qstage[qs][:, 0:2, :]),
                                reads=["qst%d" % qs])
                        vs = t % 2
                        P.op("dve", lambda e, acc=acc, vs=vs: e.tensor_copy(out=vstage[vs][:, 0:256], in_=acc[:, 256:512]),
                             reads=["acc%d" % a], writes=["vst%d" % vs])
                        P.dma("sp", "vst%d" % vs, lambda e, t=t, vs=vs: e.dma_start(
                            out=bass.AP(kvown, VA_OFF + t * 128 * 256, [[256, 128], [1, 256]]), in_=vstage[vs][:, 0:256]),
                            reads=["vst%d" % vs])
                    else:
                        vs = t % 2
                        P.op("dve", lambda e, acc=acc, vs=vs: e.tensor_copy(out=vstage[vs][:], in_=acc),
                             reads=["acc%d" % a], writes=["vst%d" % vs])
                        P.dma("sp", "vst%d" % vs, lambda e, t=t, vs=vs, g=g: e.dma_start(
                            out=bass.AP(kvown, VB_OFF + t * 128 * 1024 + (g - 9) * 512, [[1024, 128], [1, 512]]), in_=vstage[vs][:]),
                            reads=["vst%d" % vs])
            else:
                for tg in range(4):
                    fs = (g * 4 + tg) % 2
                    for cc in range(4):
                        a = next_acc()
                        acc = hb_[a]
                        for c in range(16):
                            P.op("pe", lambda e, acc=acc, c=c, cc=cc, tg=tg, s=s: e.matmul(acc, lhsT=wbuf[s][:, c, cc * 128:(cc + 1) * 128], rhs=hT[:, c, tg * 512:(tg + 1) * 512], start=(c == 0), stop=(c == 15)),
                                 reads=["wb%d" % s], writes=["acc%d" % a])
                        if g in (3, 4, 11, 12):
                            P.op("act", lambda e, acc=acc, fs=fs, cc=cc: e.activation(out=fstage[fs][:, cc, :], in_=acc, func=AF.Silu),
                                 reads=["acc%d" % a], writes=["fst%d" % fs])
                        elif cc % 2 == 0:
                            P.op("act", lambda e, acc=acc, fs=fs, cc=cc: e.activation(out=fstage[fs][:, cc, :], in_=acc, func=AF.Copy),
                                 reads=["acc%d" % a], writes=["fst%d" % fs])
                        else:
                            P.op("dve", lambda e, acc=acc, fs=fs, cc=cc: e.tensor_copy(out=fstage[fs][:, cc, :], in_=acc),
                                 reads=["acc%d" % a], writes=["fst%d" % fs])
                    if g in (3, 4):
                        dst = gT.ap()[(g - 3) * 4:(g - 3) * 4 + 4, :, tg * 512:(tg + 1) * 512].rearrange("h d t -> d h t")
                    elif g in (11, 12):
                        dst = gT.ap()[8 + (g - 11) * 4:8 + (g - 11) * 4 + 4, :, tg * 512:(tg + 1) * 512].rearrange("h d t -> d h t")
                    elif g in (5, 6):
                        dst = qbT.ap()[(g - 5) * 4:(g - 5) * 4 + 4, :, tg * 512:(tg + 1) * 512].rearrange("h d t -> d h t")
                    else:
                        r0 = 512 + (g - 7) * 512
                        dst = kvown.ap()[r0:r0 + 512, tg * 512:(tg + 1) * 512].rearrange("(h d) t -> d h t", h=4)
                    P.dma("sp", "fst%d" % fs, lambda e, dst=dst, fs=fs: e.dma_start(out=dst, in_=fstage[fs][:]), reads=["fst%d" % fs])
        P.barrier()

    for ch in range(10):
        P.cc(lambda e, ch=ch: e.collective_compute("AllGather", ALU.bypass, replica_groups=[[0, 1, 2, 3], [4, 5, 6, 7]],
                                                   ins=[kvown.ap()[ch * 256:(ch + 1) * 256, :]],
                                                   outs=[kvbig.ap()[ch * 1536 + 256:ch * 1536 + 1280, :]]), writes=["kvg%d" % ch])

    with contextlib.ExitStack() as st:
        sb = lambda name, shape, dt: st.enter_context(nc.sbuf_tensor(L + name, shape, dt))
        KT = sb("KT", [128, 2, SEQ], BF16)
        ps, _ = psum_f32(st, "pa", 4)
        V = sb("V", [128, 64, 256], BF16)
        QT = [sb("QT%d" % i, [128, 512], BF16) for i in range(2)]
        PT = [sb("PT%d" % i, [128, 1024], BF16) for i in range(3)]
        rden = sb("rden", [128, 512], F32)
        yst = [sb("yst%d" % i, [128, 512], F32) for i in range(2)]
        P.dma("pool", "kvwin", lambda e: e.dma_start(out=kvwin.ap().rearrange("(c r) n -> c r n", c=8),
                                                     in_=bass.AP(kvbig, jreg["j"] * (256 * 2048) + 2 * 1536 * 2048, [[1536 * 2048, 8], [2048, 768], [1, 2048]])),
              reads=["kvg%d" % c for c in range(2, 10)])
        for kv in range(2):
            for r in range(4):
                row0 = (1 + r) * 256 + kv * 128
                P.dma("sp", "KT", lambda e, kv=kv, r=r, row0=row0: e.dma_start(out=KT[:, kv, r * 2048:(r + 1) * 2048], in_=kvbig.ap()[row0:row0 + 128, :]), reads=["kvg0"], writes=["KT"])
        for r in range(4):
            P.dma("sp", "V", lambda e, r=r: e.dma_start(out=V[:, r * 16:(r + 1) * 16, :], in_=bass.AP(kvbig, (1536 + (1 + r) * 256) * 2048, [[256, 128], [32768, 16], [1, 256]])), reads=["kvg1"], writes=["V"])
        it = 0
        pti = 0
        for h in range(8):
            kv = h // 4
            for qg in range(4):
                qs = it % 2
                osl = it % 2
                it += 1
                O = ps[2][:, osl * 512:(osl + 1) * 512]
                Dn = ps[3][:, osl * 512:(osl + 1) * 512]
                P.dma("sp", "QT%d" % qs, lambda e, h=h, qg=qg, qs=qs: e.dma_start(out=QT[qs][:], in_=qaT.ap()[h, :, qg * 512:(qg + 1) * 512]), writes=["QT%d" % qs])

                def emit_S(kp, qs=qs, kv=kv):
                    ss_ = kp % 2
                    for u in range(2):
                        kt = 2 * kp + u
                        P.op("pe", lambda e, ss_=ss_, u=u, kt=kt: e.matmul(ps[ss_][:, u * 512:(u + 1) * 512], lhsT=KT[:, kv, kt * 128:(kt + 1) * 128], rhs=QT[qs][:], start=True, stop=True),
                             reads=["KT", "QT%d" % qs], writes=["S%d" % ss_])
                emit_S(0)
                for kp in range(32):
                    if kp + 1 < 32:
                        emit_S(kp + 1)
                    ss_ = kp % 2
                    pp = pti % 3
                    pti += 1
                    P.op("act", lambda e, ss_=ss_, pp=pp: e.activation(out=PT[pp][:], in_=ps[ss_][:], func=AF.Exp, scale=SCALE),
                         reads=["S%d" % ss_], writes=["PT%d" % pp])
                    for u in range(2):
                        kt = 2 * kp + u
                        P.op("pe", lambda e, pp=pp, u=u, kt=kt, O=O, kv=kv: e.matmul(O, lhsT=V[:, kt, kv * 128:(kv + 1) * 128], rhs=PT[pp][:, u * 512:(u + 1) * 512], start=(kt == 0), stop=(kt == 63)),
                             reads=["V", "PT%d" % pp], writes=["O%d" % osl])
                        P.op("pe", lambda e, pp=pp, u=u, kt=kt, Dn=Dn: e.matmul(Dn, lhsT=ones[:], rhs=PT[pp][:, u * 512:(u + 1) * 512], start=(kt == 0), stop=(kt == 63)),
                             reads=["ones", "PT%d" % pp], writes=["Dn%d" % osl])
                ys = osl
                P.op("dve", lambda e, Dn=Dn: e.reciprocal(out=rden[:], in_=Dn), reads=["Dn%d" % osl], writes=["rden"])
                P.op("dve", lambda e, O=O, ys=ys: e.tensor_tensor(out=yst[ys][:], in0=O, in1=rden[:], op=ALU.mult), reads=["O%d" % osl, "rden"], writes=["yst%d" % ys])
                P.dma("sp", "yst%d" % ys, lambda e, h=h, qg=qg, ys=ys: e.dma_start(out=yT.ap()[h, :, qg * 512:(qg + 1) * 512], in_=yst[ys][:]), reads=["yst%d" % ys])
        P.barrier()

    with contextlib.ExitStack() as st:
        sb = lambda name, shape, dt: st.enter_context(nc.sbuf_tensor(L + name, shape, dt))
        KB = [sb("KB%d" % i, [128, 4096], BF16) for i in range(2)]
        ps, hb_ = psum_f32(st, "pb", 4)
        VB = [sb("VB%d" % i, [128, 32, 128], BF16) for i in range(2)]
        QB = [sb("QB%d" % i, [128, TOK], BF16) for i in range(2)]
        MK = sb("MK", [128, 8, 17 * 128], BF16)
        ed = sb("ed", [128, 2], F32)
        NE, NPB, LA = 3, 4, 3
        E = [sb("E%d" % i, [128, 512], F32) for i in range(NE)]
        PB = [sb("PB%d" % i, [128, 512], BF16) for i in range(NPB)]
        rden = sb("rdenb", [128, 512], F32)
        yst = [sb("ystb%d" % i, [128, 512], F32) for i in range(2)]
        P.dma("sp", "MK", lambda e: e.dma_start(out=MK[:], in_=maskB.ap().rearrange("p (h m) -> p h m", h=8)), writes=["MK"])
        P.dma("sp", "ed", lambda e: e.dma_start(out=ed[:], in_=edge.ap()), writes=["ed"])

        def load_head(h):
            bs = h % 2
            for (so, tok0, ntok, wc0) in ((0, 1024, 1024, 0), (1, 0, 2048, 1024), (2, 0, 1024, 3072)):
                P.dma("sp", "KB%d" % bs, lambda e, h=h, bs=bs, so=so, tok0=tok0, ntok=ntok, wc0=wc0: e.dma_start(
                    out=KB[bs][:, wc0:wc0 + ntok],
                    in_=bass.AP(kvwin, ((h // 2) * 768 + so * 256 + (h % 2) * 128) * 2048 + tok0, [[2048, 128], [1, ntok]])),
                    writes=["KB%d" % bs])
                for tq in range(tok0 // 512, (tok0 + ntok) // 512):
                    wt0 = (wc0 + tq * 512 - tok0) // 128
                    P.dma("sp", "VB%d" % bs, lambda e, h=h, bs=bs, so=so, tq=tq, wt0=wt0: e.dma_start(
                        out=VB[bs][:, wt0:wt0 + 4, :],
                        in_=bass.AP(kvwin, ((4 + tq) * 768 + so * 256) * 2048 + h * 128, [[1024, 128], [131072, 4], [1, 128]])),
                        writes=["VB%d" % bs])
            P.dma("sp", "QB%d" % bs, lambda e, h=h, bs=bs: e.dma_start(out=QB[bs][:], in_=qbT.ap()[h, :, :]), writes=["QB%d" % bs])

        tasks = []
        for h in range(8):
            for qt in range(NT):
                for gi in range(5):
                    grp = list(range(qt + 4 * gi, min(qt + 4 * gi + 4, qt + 17)))
                    tasks.append((h, qt, gi, grp))

        def emit_S(ti):
            h, qt, gi, grp = tasks[ti]
            bs = h % 2
            sk = ti % 4
            S = hb_[sk]
            for i, wt in enumerate(grp):
                P.op("pe", lambda e, S=S, i=i, wt=wt, bs=bs, qt=qt: e.matmul(S[:, i * 128:(i + 1) * 128], lhsT=KB[bs][:, wt * 128:(wt + 1) * 128], rhs=QB[bs][:, qt * 128:(qt + 1) * 128], start=True, stop=True),
                     reads=["KB%d" % bs, "QB%d" % bs], writes=["SB%d" % sk])

        load_head(0)
        for ti in range(min(LA, len(tasks))):
            emit_S(ti)
        for ti, (h, qt, gi, grp) in enumerate(tasks):
            bs = h % 2
            if qt == 0 and gi == 0 and h + 1 < 8:
                load_head(h + 1)
            if ti + LA < len(tasks):
                emit_S(ti + LA)
            n = len(grp)
            sk = ti % 4
            S = hb_[sk]
            qsl = qt % 4
            osl = ((h * NT + qt) // 4) % 2
            O = ps[2][:, osl * 512 + qsl * 128: osl * 512 + (qsl + 1) * 128]
            Dn = ps[3][:, osl * 512 + qsl * 128: osl * 512 + (qsl + 1) * 128]
            es = ti % NE
            P.op("act", lambda e, S=S, es=es, n=n: e.activation(out=E[es][:, 0:n * 128], in_=S[:, 0:n * 128], func=AF.Exp, scale=SCALE),
                 reads=["SB%d" % sk], writes=["E%d" % es])
            pb = ti % NPB
            cls = ["L" if wt < 8 else ("R" if wt >= 24 else "O") for wt in grp]
            i0 = 0
            while i0 < n:
                i1 = i0
                while i1 < n and cls[i1] == cls[i0]:
                    i1 += 1
                jj0 = grp[i0] - qt
                msl = MK[:, h, jj0 * 128:(jj0 + (i1 - i0)) * 128]
                if cls[i0] == "O":
                    P.op("dve", lambda e, pb=pb, es=es, i0=i0, i1=i1, msl=msl: e.tensor_tensor(out=PB[pb][:, i0 * 128:i1 * 128], in0=E[es][:, i0 * 128:i1 * 128], in1=msl, op=ALU.mult),
                         reads=["E%d" % es, "MK"], writes=["PB%d" % pb])
                else:
                    col = 0 if cls[i0] == "L" else 1
                    P.op("dve", lambda e, pb=pb, es=es, i0=i0, i1=i1, msl=msl, col=col: e.scalar_tensor_tensor(out=PB[pb][:, i0 * 128:i1 * 128], in0=E[es][:, i0 * 128:i1 * 128], scalar=ed[:, col:col + 1], in1=msl, op0=ALU.mult, op1=ALU.mult),
                         reads=["E%d" % es, "MK", "ed"], writes=["PB%d" % pb])
                i0 = i1
            for i, wt in enumerate(grp):
                first = (wt == qt)
                lastk = (wt == qt + 16)
                P.op("pe", lambda e, O=O, pb=pb, i=i, wt=wt, bs=bs, first=first, lastk=lastk: e.matmul(O, lhsT=VB[bs][:, wt, :], rhs=PB[pb][:, i * 128:(i + 1) * 128], start=first, stop=lastk),
                     reads=["VB%d" % bs, "PB%d" % pb], writes=["OB%d" % osl])
                P.op("pe", lambda e, Dn=Dn, pb=pb, i=i, first=first, lastk=lastk: e.matmul(Dn, lhsT=ones[:], rhs=PB[pb][:, i * 128:(i + 1) * 128], start=first, stop=lastk),
                     reads=["ones", "PB%d" % pb], writes=["DB%d" % osl])
            if gi == 4 and qsl == 3:
                Of = ps[2][:, osl * 512:(osl + 1) * 512]
                Df = ps[3][:, osl * 512:(osl + 1) * 512]
                ys = osl
                qg = qt // 4
                P.op("dve", lambda e, Df=Df: e.reciprocal(out=rden[:], in_=Df), reads=["DB%d" % osl], writes=["rdenb"])
                P.op("dve", lambda e, Of=Of, ys=ys: e.tensor_tensor(out=yst[ys][:], in0=Of, in1=rden[:], op=ALU.mult), reads=["OB%d" % osl, "rdenb"], writes=["ystb%d" % ys])
                P.dma("sp", "ystb%d" % ys, lambda e, h=h, qg=qg, ys=ys: e.dma_start(out=yT.ap()[8 + h, :, qg * 512:(qg + 1) * 512], in_=yst[ys][:]), reads=["ystb%d" % ys])
        P.barrier()

    with contextlib.ExitStack() as st:
        sb = lambda name, shape, dt: st.enter_context(nc.sbuf_tensor(L + name, shape, dt))
        WO = sb("WO", [128, 16, D], BF16)
        ps, hb_ = psum_f32(st, "po", 3)
        yt = [sb("yt%d" % i, [128, 16, 256], F32) for i in range(2)]
        gt = [sb("gt%d" % i, [128, 16, 256], BF16) for i in range(2)]
        zt = [sb("zt%d" % i, [128, 16, 256], BF16) for i in range(2)]
        sqo = [sb("sqo%d" % i, [128, 256], F32) for i in range(2)]
        tmp = [sb("tmp%d" % i, [128, 256], F32) for i in range(2)]
        rstd = [sb("rstd%d" % i, [128, 256], F32) for i in range(2)]
        onw = sb("onw", [128, 16], F32)
        xo = [sb("xo%d" % i, [128, D], F32) for i in range(2)]
        fw = sb("fw", [128, D], F32)
        sqf = sb("sqf", [128, D], BF16)
        ssf = sb("ssf", [128, 2], F32)
        for half in range(2):
            P.dma("pool", "WO", lambda e, half=half: e.dma_start(out=WO[:, half * 8:(half + 1) * 8, :], in_=w_out.ap()[li, half * 1024:(half + 1) * 1024, :].rearrange("(c p) n -> p c n", p=128)), writes=["WO"])
        P.dma("sp", "onw", lambda e: e.dma_start(out=onw[:, 0:8], in_=bass.AP(ona, li * 1024, [[1, 128], [128, 8]]), allow_slow_non_contiguous=True), writes=["onw"])
        P.dma("sp", "onw", lambda e: e.dma_start(out=onw[:, 8:16], in_=bass.AP(onb, li * 1024, [[1, 128], [128, 8]]), allow_slow_non_contiguous=True), writes=["onw"])
        if do_final:
            P.dma("sp", "fw", lambda e: e.dma_start(out=fw[:], in_=bcast_row(fnw, 0, D)), writes=["fw"])
        oacc = 0
        for tg in range(8):
            s = tg % 2
            c0 = tg * 256
            P.dma("sp", "yt%d" % s, lambda e, s=s, c0=c0: e.dma_start(out=yt[s][:], in_=yT.ap()[:, :, c0:c0 + 256].rearrange("c d t -> d c t")), writes=["yt%d" % s])
            P.dma("sp", "gt%d" % s, lambda e, s=s, c0=c0: e.dma_start(out=gt[s][:], in_=gT.ap()[:, :, c0:c0 + 256].rearrange("c d t -> d c t")), writes=["gt%d" % s])
            for mix in range(2):
                ssp = hb_[4 + mix]
                for c in range(8):
                    ch = mix * 8 + c
                    q = ch % 2
                    P.op("act", lambda e, s=s, ch=ch, q=q: e.activation(out=sqo[q][:], in_=yt[s][:, ch, :], func=AF.Square),
                         reads=["yt%d" % s], writes=["sqo%d" % q])
                    P.op("pe", lambda e, ssp=ssp, q=q, c=c: e.matmul(ssp[:, 0:256], lhsT=onesf[:], rhs=sqo[q][:], start=(c == 0), stop=(c == 7)),
                         reads=["onesf", "sqo%d" % q], writes=["ssp%d" % mix])
                P.op("act", lambda e, ssp=ssp, mix=mix: e.activation(out=rstd[mix][:], in_=ssp[:, 0:256], func=AF.Sqrt, scale=1.0 / 1024, bias=epsb[:]),
                     reads=["ssp%d" % mix, "epsb"], writes=["rstd%d" % mix])
                P.op("dve", lambda e, mix=mix: e.reciprocal(out=rstd[mix][:], in_=rstd[mix][:]),
                     reads=["rstd%d" % mix], writes=["rstd%d" % mix])
                for c in range(8):
                    ch = mix * 8 + c
                    q = ch % 2
                    P.op("dve", lambda e, s=s, ch=ch, q=q, mix=mix: e.tensor_tensor(out=tmp[q][:], in0=yt[s][:, ch, :], in1=rstd[mix][:], op=ALU.mult),
                         reads=["yt%d" % s, "rstd%d" % mix], writes=["tmp%d" % q])
                    P.op("dve", lambda e, s=s, ch=ch, q=q: e.scalar_tensor_tensor(out=zt[s][:, ch, :], in0=gt[s][:, ch, :], scalar=onw[:, ch:ch + 1], in1=tmp[q][:], op0=ALU.mult, op1=ALU.mult),
                         reads=["gt%d" % s, "onw", "tmp%d" % q], writes=["zt%d" % s])
            for tt in range(2):
                t = tg * 2 + tt
                xs = t % 2
                P.dma("sp", "xo%d" % xs, lambda e, t=t, xs=xs: e.dma_start(out=xo[xs][:], in_=x_src.ap()[t * 128:(t + 1) * 128, :]), writes=["xo%d" % xs])
                for cg in range(4):
                    a = oacc % 4
                    oacc += 1
                    acc = hb_[a]
                    for c in range(16):
                        P.op("pe", lambda e, acc=acc, c=c, s=s, tt=tt, cg=cg: e.matmul(acc, lhsT=zt[s][:, c, tt * 128:(tt + 1) * 128], rhs=WO[:, c, cg * 512:(cg + 1) * 512], start=(c == 0), stop=(c == 15)),
                             reads=["zt%d" % s, "WO"], writes=["oacc%d" % a])
                    P.op("dve", lambda e, acc=acc, xs=xs, cg=cg: e.tensor_tensor(out=xo[xs][:, cg * 512:(cg + 1) * 512], in0=acc, in1=xo[xs][:, cg * 512:(cg + 1) * 512], op=ALU.add),
                         reads=["oacc%d" % a, "xo%d" % xs], writes=["xo%d" % xs])
                if do_final:
                    P.op("act", lambda e, xs=xs: e.activation(out=sqf[:], in_=xo[xs][:], func=AF.Square, accum_out=ssf[:, xs:xs + 1]),
                         reads=["xo%d" % xs], writes=["sqf", "ssf%d" % xs])
                    P.op("act", lambda e, xs=xs: e.activation(out=ssf[:, xs:xs + 1], in_=ssf[:, xs:xs + 1], func=AF.Sqrt, scale=1.0 / D, bias=epsb[:]),
                         reads=["ssf%d" % xs, "epsb"], writes=["ssf%d" % xs])
                    P.op("dve", lambda e, xs=xs: e.reciprocal(out=ssf[:, xs:xs + 1], in_=ssf[:, xs:xs + 1]),
                         reads=["ssf%d" % xs], writes=["ssf%d" % xs])
                    P.op("dve", lambda e, xs=xs: e.scalar_tensor_tensor(out=xo[xs][:], in0=xo[xs][:], scalar=ssf[:, xs:xs + 1], in1=fw[:], op0=ALU.mult, op1=ALU.mult),
                         reads=["xo%d" % xs, "ssf%d" % xs, "fw"], writes=["xo%d" % xs])
                P.dma("sp", "xo%d" % xs, lambda e, t=t, xs=xs: e.dma_start(out=x_dst.ap()[t * 128:(t + 1) * 128, :], in_=xo[xs][:]), reads=["xo%d" % xs])
        P.barrier()


def _rope_tables():
    t = np.arange(SEQ)
    row = (t // 64).astype(np.float32)
    col = (t % 64).astype(np.float32)
    inv = (10000.0 ** (-np.arange(0, 64, 2, dtype=np.float32) / 64)).astype(np.float32)
    ar = row[:, None] * inv[None, :]
    ac = col[:, None] * inv[None, :]
    cr, sr, cc, sc = np.cos(ar), np.sin(ar), np.cos(ac), np.sin(ac)
    C = np.concatenate([cr, cr, cc, cc], axis=1).astype(np.float32)
    S = np.concatenate([-sr, sr, -sc, sc], axis=1).astype(np.float32)
    return C, S


def _mask_b():
    slopes = (2.0 ** -np.arange(1, 9)).astype(np.float64)
    k = np.arange(128)[:, None]
    q = np.arange(128)[None, :]
    M = np.zeros((128, 8, 17, 128), np.float64)
    for jj in range(17):
        off = (jj - 8) * 128 + k - q
        a = np.abs(off)
        mult = np.zeros_like(off, dtype=np.float64)
        for w, d in PATTERNS:
            mult += ((a <= w // 2) & (off % d == 0)).astype(np.float64)
        for h in range(8):
            M[:, h, jj, :] = mult * np.exp(-slopes[h] * a)
    return M.reshape(128, 8 * 17 * 128).astype(ml_dtypes.bfloat16)


_CACHE = {}


def _get_prog(nl, first, last):
    key = (nl, last)
    if key not in _CACHE:
        _CACHE[key] = build_program(nl, first, last)
    return _CACHE[key]


def _consts():
    if "c" not in _CACHE:
        C, S = _rope_tables()
        _CACHE["c"] = (C, S, _mask_b(), np.eye(128, dtype=np.float32).astype(ml_dtypes.bfloat16))
    return _CACHE["c"]


def run_layers(xs, layers, first, last, w):
    nl = len(layers)
    nc = _get_prog(nl, first, last)
    C, S, M, I = _consts()
    f = lambda a: np.ascontiguousarray(a, dtype=np.float32)
    sel = lambda a: f(a[layers])
    in_maps = []
    for c in range(8):
        j = c % 4
        edge = np.zeros((128, 2), np.float32)
        edge[:, 0] = 1.0 if j > 0 else 0.0
        edge[:, 1] = 1.0 if j < 3 else 0.0
        in_maps.append({
            "x": f(xs[c]), "w_in": sel(w["w_in"]), "w_out": sel(w["w_out"]), "norm_w": sel(w["norm_w"]),
            "q_norm_a": sel(w["q_norm_a"]), "k_norm_a": sel(w["k_norm_a"]),
            "out_norm_a": sel(w["out_norm_a"]), "out_norm_b": sel(w["out_norm_b"]),
            "final_norm": f(w["final_norm"]).reshape(1, D),
            "ropeC": f(C[j * TOK:(j + 1) * TOK]), "ropeS": f(S[j * TOK:(j + 1) * TOK]),
            "maskB": M, "edge": edge, "ident": I,
        })
    res = run_bass_kernel_spmd(nc, in_maps, core_ids=list(range(8)))
    return [r["out"] for r in res.results]


LAUNCH_GROUPS = [[0, 1, 2, 3]]


def kernel(x, norm_w, w_in, q_norm_a, k_norm_a, out_norm_a, out_norm_b, w_out, final_norm):
    w = dict(norm_w=np.asarray(norm_w), w_in=np.asarray(w_in), q_norm_a=np.asarray(q_norm_a), k_norm_a=np.asarray(k_norm_a),
             out_norm_a=np.asarray(out_norm_a), out_norm_b=np.asarray(out_norm_b), w_out=np.asarray(w_out), final_norm=np.asarray(final_norm))
    x = np.asarray(x)
    xs = [x[c // 4, (c % 4) * TOK:(c % 4 + 1) * TOK, :] for c in range(8)]
    ng = len(LAUNCH_GROUPS)
    for gi, grp in enumerate(LAUNCH_GROUPS):
        xs = run_layers(xs, grp, gi == 0, gi == ng - 1, w)
    out = np.empty((2, SEQ, D), np.float32)
    for c in range(8):
        out[c // 4, (c % 4) * TOK:(c % 4 + 1) * TOK, :] = xs[c]
    return out
```

```python
import contextlib
import numpy as np
import ml_dtypes
import concourse.bass as bass
import concourse.mybir as mybir
from concourse.bass_utils import run_bass_kernel_spmd

F32 = mybir.dt.float32
BF16 = mybir.dt.bfloat16
ALU = mybir.AluOpType
AF = mybir.ActivationFunctionType
AX = mybir.AxisListType

ENGS = ["pe", "act", "dve", "pool", "sp"]

D = 2048
SEQ = 8192
TOK = 2048
NT = TOK // 128
DIN = 6656
HD = 128
EPS = 1e-6
SCALE = HD ** -0.5
RANK = 2560 * 2048
KA_OFF = 0
VA_OFF = 256 * 2048
KB_OFF = 512 * 2048
VB_OFF = 1536 * 2048
PATTERNS = ((128, 1), (512, 4), (2048, 16))


class Prog:
    def __init__(self, nc, n_dma_sems=110):
        self.nc = nc
        self.segments = []
        self.cur = {e: [] for e in ENGS}
        self.cnt = {e: 0 for e in ENGS}
        self.waited = {e: {} for e in ENGS}
        self.bufs = {}
        self.dtot = {}
        self.n_dma_sems = n_dma_sems
        self.dma_keys = {}
        self.nbar = 0
        self.same_eng_wait = {"pe": False, "act": True, "dve": True, "pool": True, "sp": False}

    def _deps(self, eng, reads, writes):
        evs = []
        for k in reads:
            b = self.bufs.get(k)
            if b and b[0] is not None:
                evs.append(b[0])
        for k in writes:
            b = self.bufs.get(k)
            if b:
                if b[0] is not None:
                    evs.append(b[0])
                evs.extend(b[1])
        need = {}
        for (sk, v) in evs:
            if sk == eng and not self.same_eng_wait[eng]:
                continue
            if self.waited[eng].get(sk, 0) >= v:
                continue
            if need.get(sk, 0) < v:
                need[sk] = v
        for sk, v in need.items():
            self.waited[eng][sk] = v
        return list(need.items())

    def _record(self, ev, reads, writes):
        for k in reads:
            self.bufs.setdefault(k, [None, []])[1].append(ev)
        for k in writes:
            self.bufs[k] = [ev, []]

    def op(self, eng, fn, reads=(), writes=()):
        waits = self._deps(eng, reads, writes)
        self.cnt[eng] += 1
        ev = (eng, self.cnt[eng])
        self.cur[eng].append(("op", fn, waits, ev))
        self._record(ev, reads, writes)
        return ev

    def dma(self, eng, semkey, fn, reads=(), writes=()):
        if semkey not in self.dma_keys:
            assert len(self.dma_keys) < self.n_dma_sems, "out of dma sems"
            self.dma_keys[semkey] = len(self.dma_keys)
        waits = self._deps(eng, reads, writes)
        self.dtot[semkey] = self.dtot.get(semkey, 0) + 16
        ev = (semkey, self.dtot[semkey])
        self.cur[eng].append(("dma", fn, waits, ev))
        self._record(ev, reads, writes)
        return ev

    def cc(self, fn, writes=()):
        semkey = "_cc"
        if semkey not in self.dma_keys:
            self.dma_keys[semkey] = len(self.dma_keys)
        self.dtot[semkey] = self.dtot.get(semkey, 0) + 1
        ev = (semkey, self.dtot[semkey])
        self.cur["pool"].append(("cc", fn, [], ev))
        self._record(ev, (), writes)

    def barrier(self):
        self.nbar += 1
        self.cur["_bar"] = dict(cnt=dict(self.cnt), dtot=dict(self.dtot), idx=self.nbar)
        self.segments.append(self.cur)
        self.cur = {e: [] for e in ENGS}
        self.bufs = {}

    def emit(self):
        nc = self.nc
        if any(self.cur[e] for e in ENGS):
            self.barrier()
        segs = self.segments
        needed = {e: set() for e in ENGS}
        for seg in segs:
            for en in ENGS:
                for (kind, fn, waits, ev) in seg[en]:
                    for (sk, v) in waits:
                        if sk in needed:
                            needed[sk].add(v)
                if seg["_bar"]["cnt"][en] > 0:
                    needed[en].add(seg["_bar"]["cnt"][en])
        rank = {e: {v: i + 1 for i, v in enumerate(sorted(needed[e]))} for e in ENGS}

        def wval(sk, v):
            return rank[sk][v] if sk in rank else v
        with contextlib.ExitStack() as st:
            esem = {e: st.enter_context(nc.semaphore("s_" + e)) for e in ENGS}
            dsem = [st.enter_context(nc.semaphore("d%d" % i)) for i in range(len(self.dma_keys))]
            b1 = st.enter_context(nc.semaphore("bar1"))
            b2 = st.enter_context(nc.semaphore("bar2"))
            block = st.enter_context(nc.Block())

            def sem_of(sk):
                if sk in esem:
                    return esem[sk]
                return dsem[self.dma_keys[sk]]

            def run(eng_name, e):
                for seg in segs:
                    for (kind, fn, waits, ev) in seg[eng_name]:
                        for (sk, v) in waits:
                            e.wait_ge(sem_of(sk), wval(sk, v))
                        ins = fn(e)
                        if kind == "op":
                            if ev[1] in needed[eng_name]:
                                ins.then_inc(esem[eng_name], 1)
                        elif kind == "dma":
                            ins.then_inc(sem_of(ev[0]), 16)
                        elif kind == "cc":
                            ins.then_inc(sem_of(ev[0]), 1)
                    bar = seg["_bar"]
                    if bar["cnt"][eng_name] > 0:
                        e.wait_ge(esem[eng_name], wval(eng_name, bar["cnt"][eng_name]))
                    if eng_name != "pool":
                        e.sem_inc(b1, 1)
                        e.wait_ge(b2, bar["idx"])
                    else:
                        e.wait_ge(b1, (len(ENGS) - 1) * bar["idx"])
                        for sk, v in bar["dtot"].items():
                            e.wait_ge(sem_of(sk), v)
                        e.sem_inc(b2, 1)

            @block.tensor
            def _(e):
                run("pe", e)

            @block.scalar
            def _(e):
                run("act", e)

            @block.vector
            def _(e):
                run("dve", e)

            @block.gpsimd
            def _(e):
                self.pool_eng_setup(e)
                run("pool", e)

            @block.sync
            def _(e):
                run("sp", e)

    def pool_eng_setup(self, e):
        pass


def build_program(nl, first, last):
    nc = bass.Bass("TRN2", target_bir_lowering=False)
    dt_in = lambda name, shape, dt=F32: nc.dram_tensor(name, shape, dt, kind="ExternalInput")
    x_in = dt_in("x", [TOK, D])
    w_in = dt_in("w_in", [nl, D, DIN])
    w_out = dt_in("w_out", [nl, D, D])
    norm_w = dt_in("norm_w", [nl, D])
    qn = dt_in("q_norm_a", [nl, HD])
    kn = dt_in("k_norm_a", [nl, HD])
    ona = dt_in("out_norm_a", [nl, 1024])
    onb = dt_in("out_norm_b", [nl, 1024])
    fnw = dt_in("final_norm", [1, D])
    ropeC = dt_in("ropeC", [TOK, HD])
    ropeS = dt_in("ropeS", [TOK, HD])
    maskB = dt_in("maskB", [128, 8 * 17 * 128], BF16)
    edge = dt_in("edge", [128, 2])
    ident_d = dt_in("ident", [128, 128], BF16)
    out_d = nc.dram_tensor("out", [TOK, D], F32, kind="ExternalOutput")

    xres = [nc.dram_tensor("xres%d" % i, [TOK, D], F32) for i in range(2)]
    qaT = nc.dram_tensor("qaT", [8, 128, TOK], BF16)
    qbT = nc.dram_tensor("qbT", [8, 128, TOK], BF16)
    gT = nc.dram_tensor("gT", [16, 128, TOK], BF16)
    yT = nc.dram_tensor("yT", [16, 128, TOK], F32)
    kvown = nc.dram_tensor("kvown", [2560, 2048], BF16)
    kvbig = nc.dram_tensor("kvbig", [10 * 1536, 2048], BF16)
    kvwin = nc.dram_tensor("kvwin", [8 * 768, 2048], BF16)

    P = Prog(nc)
    jreg = {}

    def setup_pool(e):
        pid = e.partition_id()
        jreg["j"] = pid % 4
    P.pool_eng_setup = setup_pool

    with contextlib.ExitStack() as gst:
        ident = gst.enter_context(nc.sbuf_tensor("ident_sb", [128, 128], BF16))
        ones = gst.enter_context(nc.sbuf_tensor("ones", [128, 128], BF16))
        onesf = gst.enter_context(nc.sbuf_tensor("onesf", [128, 128], F32))
        zer = gst.enter_context(nc.sbuf_tensor("zer", [128, 2048], BF16))
        P.dma("sp", "ident", lambda e: e.dma_start(out=ident[:], in_=ident_d.ap()), writes=["ident"])
        epsb = gst.enter_context(nc.sbuf_tensor("epsb", [128, 1], F32))
        P.op("dve", lambda e: e.memset(epsb[:], EPS), writes=["epsb"])
        P.op("dve", lambda e: e.memset(ones[:], 1.0), writes=["ones"])
        P.op("dve", lambda e: e.memset(onesf[:], 1.0), writes=["onesf"])
        P.op("dve", lambda e: e.memset(zer[:], 0.0), writes=["zer"])
        for ch in range(10):
            for slot in (0, 5):
                for part in range(2):
                    r0 = ch * 1536 + slot * 256 + part * 128
                    P.dma("sp", "zpad", lambda e, r0=r0: e.dma_start(out=kvbig.ap()[r0:r0 + 128, :], in_=zer[:]), reads=["zer"])
        P.barrier()

        for li in range(nl):
            x_src = x_in if (li == 0 and first) else xres[(li - 1) % 2]
            if li == 0 and not first:
                x_src = x_in
            is_last = (li == nl - 1)
            x_dst = out_d if is_last else xres[li % 2]
            layer(P, nc, li, x_src, x_dst, is_last and last, locals())
        P.emit()
    return nc


def layer(P, nc, li, x_src, x_dst, do_final, G):
    def psum_f32(stk, name, n):
        tl = [stk.enter_context(nc.psum_tensor(L + name + str(i), [128, 1024], F32)) for i in range(n)]
        halves = [tl[i // 2][:, (i % 2) * 512:(i % 2 + 1) * 512] for i in range(2 * n)]
        return tl, halves
    ident, ones, onesf, epsb = G["ident"], G["ones"], G["onesf"], G["epsb"]
    w_in, w_out, norm_w, qn, kn, ona, onb, fnw = (G[k] for k in ["w_in", "w_out", "norm_w", "qn", "kn", "ona", "onb", "fnw"])
    ropeC, ropeS, maskB, edge = G["ropeC"], G["ropeS"], G["maskB"], G["edge"]
    qaT, qbT, gT, yT, kvown, kvbig = G["qaT"], G["qbT"], G["gT"], G["yT"], G["kvown"], G["kvbig"]
    kvwin = G["kvwin"]
    jreg = G["jreg"]
    L = "L%d" % li

    def bcast_row(handle, row, n):
        return bass.AP(handle, row * n, [[0, 128], [1, n]])

    with contextlib.ExitStack() as st:
        sb = lambda name, shape, dt: st.enter_context(nc.sbuf_tensor(L + name, shape, dt))
        hT = sb("hT", [128, 16, TOK], BF16)
        wbuf = [sb("wb%d" % i, [128, 16, 512], BF16) for i in range(2)]

        def load_w(g):
            s_ = g % 2
            P.dma("pool", "wb%d" % s_, lambda e: e.dma_start(out=wbuf[s_][:], in_=w_in.ap()[li, :, g * 512:(g + 1) * 512].rearrange("(c p) n -> p c n", p=128)),
                  writes=["wb%d" % s_])
        load_w(0)
        load_w(1)
        with contextlib.nullcontext():
            sb1 = sb
            xt = [sb1("xt%d" % i, [128, D], F32) for i in range(2)]
            hbt = [sb1("hbt%d" % i, [128, D], BF16) for i in range(2)]
            sq = sb1("sq", [128, D], F32)
            ss = sb1("ss", [128, 16], F32)
            rs = sb1("rs", [128, 16], F32)
            nw = sb1("nw", [128, D], F32)
            tpp = [st.enter_context(nc.psum_tensor(L + "tpp%d" % i, [128, 1024], BF16)) for i in range(2)]
            P.dma("sp", "nw", lambda e: e.dma_start(out=nw[:], in_=bcast_row(norm_w, li, D)), writes=["nw"])
            for t in range(NT):
                s = t % 2
                P.dma("sp", "xt%d" % s, lambda e, t=t, s=s: e.dma_start(out=xt[s][:], in_=x_src.ap()[t * 128:(t + 1) * 128, :]), writes=["xt%d" % s])
                P.op("act", lambda e, t=t, s=s: e.activation(out=sq[:], in_=xt[s][:], func=AF.Square, accum_out=ss[:, t:t + 1]),
                     reads=["xt%d" % s], writes=["sq", "ss%d" % t])
                P.op("act", lambda e, t=t: e.activation(out=rs[:, t:t + 1], in_=ss[:, t:t + 1], func=AF.Sqrt, scale=1.0 / D, bias=epsb[:]),
                     reads=["ss%d" % t, "epsb"], writes=["rs%d" % t])
                P.op("dve", lambda e, t=t: e.reciprocal(out=rs[:, t:t + 1], in_=rs[:, t:t + 1]),
                     reads=["rs%d" % t], writes=["rs%d" % t])
                P.op("dve", lambda e, t=t, s=s: e.scalar_tensor_tensor(out=hbt[s][:], in0=xt[s][:], scalar=rs[:, t:t + 1], in1=nw[:], op0=ALU.mult, op1=ALU.mult),
                     reads=["xt%d" % s, "rs%d" % t, "nw"], writes=["hbt%d" % s])
                for half in range(2):
                    pt = tpp[half][:]
                    for c in range(8):
                        P.op("pe", lambda e, s=s, half=half, c=c, pt=pt: e.transpose(out=pt[:, c * 128:(c + 1) * 128], in_=hbt[s][:, (half * 8 + c) * 128:(half * 8 + c + 1) * 128], identity=ident[:]),
                             reads=["hbt%d" % s, "ident"], writes=["tp%d" % half])
                    eng = "act" if half == 0 else "dve"
                    if eng == "act":
                        P.op("act", lambda e, t=t, half=half, pt=pt: e.activation(out=hT[:, half * 8:(half + 1) * 8, t * 128:(t + 1) * 128], in_=pt.rearrange("p (c k) -> p c k", k=128), func=AF.Copy),
                             reads=["tp%d" % half], writes=["hTa%d" % t])
                    else:
                        P.op("dve", lambda e, t=t, half=half, pt=pt: e.tensor_copy(out=hT[:, half * 8:(half + 1) * 8, t * 128:(t + 1) * 128], in_=pt.rearrange("p (c k) -> p c k", k=128)),
                             reads=["tp%d" % half], writes=["hTb%d" % t])

        _, hb_ = psum_f32(st, "acc", 2)
        tpq = st.enter_context(nc.psum_tensor(L + "tpq", [128, 512], BF16))
        Ct = sb("Ct", [128, NT, HD], F32)
        St = sb("St", [128, NT, HD], F32)
        gq = sb("gq", [128, HD], F32)
        gk = sb("gk", [128, HD], F32)
        junk = sb("junk", [128, HD], F32)
        ssq = sb("ssq", [128, 4], F32)
        rsq = sb("rsq", [128, 4], F32)
        xn = [sb("xn%d" % i, [128, HD], F32) for i in range(2)]
        t1 = [sb("t1%d" % i, [128, HD], F32) for i in range(2)]
        t2 = [sb("t2%d" % i, [128, HD], F32) for i in range(2)]
        ro = [sb("ro%d" % i, [128, HD], BF16) for i in range(2)]
        qstage = [sb("qst%d" % i, [128, 4, 512], BF16) for i in range(2)]
        fstage = [sb("fst%d" % i, [128, 4, 512], BF16) for i in range(2)]
        vstage = [sb("vst%d" % i, [128, 512], BF16) for i in range(2)]
        P.dma("sp", "Ct", lambda e: e.dma_start(out=Ct[:], in_=ropeC.ap().rearrange("(t p) d -> p t d", p=128)), writes=["Ct"])
        P.dma("sp", "St", lambda e: e.dma_start(out=St[:], in_=ropeS.ap().rearrange("(t p) d -> p t d", p=128)), writes=["St"])
        P.dma("sp", "gq", lambda e: e.dma_start(out=gq[:], in_=bcast_row(qn, li, HD)), writes=["gq"])
        P.dma("sp", "gk", lambda e: e.dma_start(out=gk[:], in_=bcast_row(kn, li, HD)), writes=["gk"])

        T_GROUPS = {0, 1, 2, 9, 10}
        acc_i = [0]
        tp_i = [0]
        hd_i = [0]

        def next_acc():
            a = acc_i[0] % 4
            acc_i[0] += 1
            return a

        def rope_head(acc_ap, accs_key, t, gain, out_stage, out_key, hslot, tcol):
            i = hd_i[0] % 2
            hd_i[0] += 1
            k = "h%d" % i
            P.op("act", lambda e: e.activation(out=junk[:], in_=acc_ap, func=AF.Square, accum_out=ssq[:, i:i + 1]),
                 reads=[accs_key], writes=["junk", "ssq" + k])
            P.op("act", lambda e: e.activation(out=rsq[:, i:i + 1], in_=ssq[:, i:i + 1], func=AF.Sqrt, scale=1.0 / HD, bias=epsb[:]),
                 reads=["ssq" + k, "epsb"], writes=["rsq" + k])
            P.op("dve", lambda e: e.reciprocal(out=rsq[:, i:i + 1], in_=rsq[:, i:i + 1]),
                 reads=["rsq" + k], writes=["rsq" + k])
            P.op("dve", lambda e: e.scalar_tensor_tensor(out=xn[i][:], in0=acc_ap, scalar=rsq[:, i:i + 1], in1=gain[:], op0=ALU.mult, op1=ALU.mult),
                 reads=[accs_key, "rsq" + k, "gq", "gk"], writes=["xn" + k])
            P.op("dve", lambda e: e.tensor_tensor(out=t1[i][:], in0=xn[i][:], in1=Ct[:, t, :], op=ALU.mult),
                 reads=["xn" + k, "Ct"], writes=["t1" + k])
            xv = xn[i][:].rearrange("p (a h c) -> p a h c", a=2, h=2)
            sv = St[:, t, :].rearrange("p (a h c) -> p a h c", a=2, h=2)
            tv = t2[i][:].rearrange("p (a h c) -> p a h c", a=2, h=2)
            P.op("dve", lambda e: e.tensor_tensor(out=tv[:, :, 0, :], in0=xv[:, :, 1, :], in1=sv[:, :, 0, :], op=ALU.mult),
                 reads=["xn" + k, "St"], writes=["t2a" + k])
            P.op("dve", lambda e: e.tensor_tensor(out=tv[:, :, 1, :], in0=xv[:, :, 0, :], in1=sv[:, :, 1, :], op=ALU.mult),
                 reads=["xn" + k, "St"], writes=["t2b" + k])
            P.op("dve", lambda e: e.tensor_tensor(out=ro[i][:], in0=t1[i][:], in1=t2[i][:], op=ALU.add),
                 reads=["t1" + k, "t2a" + k, "t2b" + k], writes=["ro" + k])
            tpi = tp_i[0] % 4
            tp_i[0] += 1
            pt = tpq[:, tpi * 128:(tpi + 1) * 128]
            P.op("pe", lambda e: e.transpose(out=pt, in_=ro[i][:], identity=ident[:]),
                 reads=["ro" + k, "ident"], writes=["tpq%d" % tpi])
            P.op("act", lambda e: e.activation(out=out_stage[:, hslot, tcol * 128:(tcol + 1) * 128], in_=pt, func=AF.Copy),
                 reads=["tpq%d" % tpi], writes=[out_key])

        for g in range(13):
            s = g % 2
            if g in T_GROUPS:
                for t in range(NT):
                    a = next_acc()
                    acc = hb_[a]
                    for c in range(16):
                        P.op("pe", lambda e, acc=acc, c=c, t=t, s=s: e.matmul(acc, lhsT=hT[:, c, t * 128:(t + 1) * 128], rhs=wbuf[s][:, c, :], start=(c == 0), stop=(c == 15)),
                             reads=["wb%d" % s, "hTa%d" % t, "hTb%d" % t], writes=["acc%d" % a])
                    tg, tc = t // 4, t % 4
                    if g in (0, 1):
                        qs = (g * 4 + tg) % 2
                        for hh in range(4):
                            rope_head(acc[:, hh * 128:(hh + 1) * 128], "acc%d" % a, t, gq, qstage[qs], "qst%d" % qs, hh, tc)
                        if tc == 3:
                            P.dma("sp", "qst%d" % qs, lambda e, g=g, tg=tg, qs=qs: e.dma_start(
                                out=qaT.ap()[g * 4:(g + 1) * 4, :, tg * 512:(tg + 1) * 512].rearrange("h d t -> d h t"), in_=qstage[qs][:]),
                                reads=["qst%d" % qs])
                    elif g == 2:
                        qs = tg % 2
                        for hh in range(2):
                            rope_head(acc[:, hh * 128:(hh + 1) * 128], "acc%d" % a, t, gk, qstage[qs], "qst%d" % qs, hh, tc)
                        if tc == 3:
                            P.dma("sp", "qst%d" % qs, lambda e, tg=tg, qs=qs: e.dma_start(
                                out=kvown.ap()[0:256, tg * 512:(tg + 1) * 512].rearrange("(h d) t -> d h t", h=2), in_=qstage[qs][:, 0:2, :]),
                                reads=["qst%d" % qs])
                        vs = t % 2
                        P.op("dve", lambda e, acc=acc, vs=vs: e.tensor_copy(out=vstage[vs][:, 0:256], in_=acc[:, 256:512]),
                             reads=["acc%d" % a], writes=["vst%d" % vs])
                        P.dma("sp", "vst%d" % vs, lambda e, t=t, vs=vs: e.dma_start(
                            out=bass.AP(kvown, VA_OFF + t * 128 * 256, [[256, 128], [1, 256]]), in_=vstage[vs][:, 0:256]),
                            reads=["vst%d" % vs])
                    else:
                        vs = t % 2
                        P.op("dve", lambda e, acc=acc, vs=vs: e.tensor_copy(out=vstage[vs][:], in_=acc),
                             reads=["acc%d" % a], writes=["vst%d" % vs])
                        P.dma("sp", "vst%d" % vs, lambda e, t=t, vs=vs, g=g: e.dma_start(
                            out=bass.AP(kvown, VB_OFF + t * 128 * 1024 + (g - 9) * 512, [[1024, 128], [1, 512]]), in_=vstage[vs][:]),
                            reads=["vst%d" % vs])
            else:
                for tg in range(4):
                    fs = (g * 4 + tg) % 2
                    for cc in range(4):
                        a = next_acc()
                        acc = hb_[a]
                        for c in range(16):
                            P.op("pe", lambda e, acc=acc, c=c, cc=cc, tg=tg, s=s: e.matmul(acc, lhsT=wbuf[s][:, c, cc * 128:(cc + 1) * 128], rhs=hT[:, c, tg * 512:(tg + 1) * 512], start=(c == 0), stop=(c == 15)),
                                 reads=["wb%d" % s] + ["hT%s%d" % (ab, tg * 4 + q4) for ab in "ab" for q4 in range(4)], writes=["acc%d" % a])
                        if g in (3, 4, 11, 12):
                            P.op("act", lambda e, acc=acc, fs=fs, cc=cc: e.activation(out=fstage[fs][:, cc, :], in_=acc, func=AF.Silu),
                                 reads=["acc%d" % a], writes=["fst%d" % fs])
                        elif cc % 2 == 0:
                            P.op("act", lambda e, acc=acc, fs=fs, cc=cc: e.activation(out=fstage[fs][:, cc, :], in_=acc, func=AF.Copy),
                                 reads=["acc%d" % a], writes=["fst%d" % fs])
                        else:
                            P.op("dve", lambda e, acc=acc, fs=fs, cc=cc: e.tensor_copy(out=fstage[fs][:, cc, :], in_=acc),
                                 reads=["acc%d" % a], writes=["fst%d" % fs])
                    if g in (3, 4):
                        dst = gT.ap()[(g - 3) * 4:(g - 3) * 4 + 4, :, tg * 512:(tg + 1) * 512].rearrange("h d t -> d h t")
                    elif g in (11, 12):
                        dst = gT.ap()[8 + (g - 11) * 4:8 + (g - 11) * 4 + 4, :, tg * 512:(tg + 1) * 512].rearrange("h d t -> d h t")
                    elif g in (5, 6):
                        dst = qbT.ap()[(g - 5) * 4:(g - 5) * 4 + 4, :, tg * 512:(tg + 1) * 512].rearrange("h d t -> d h t")
                    else:
                        r0 = 512 + (g - 7) * 512
                        dst = kvown.ap()[r0:r0 + 512, tg * 512:(tg + 1) * 512].rearrange("(h d) t -> d h t", h=4)
                    P.dma("sp", "fst%d" % fs, lambda e, dst=dst, fs=fs: e.dma_start(out=dst, in_=fstage[fs][:]), reads=["fst%d" % fs])
            if g + 2 < 13:
                load_w(g + 2)
        P.barrier()

    for ch in range(10):
        P.cc(lambda e, ch=ch: e.collective_compute("AllGather", ALU.bypass, replica_groups=[[0, 1, 2, 3], [4, 5, 6, 7]],
                                                   ins=[kvown.ap()[ch * 256:(ch + 1) * 256, :]],
                                                   outs=[kvbig.ap()[ch * 1536 + 256:ch * 1536 + 1280, :]]), writes=["kvg%d" % ch])

    st_abo = contextlib.ExitStack()
    st_ab = contextlib.ExitStack()
    KTV = st_abo.enter_context(nc.sbuf_tensor(L + "KTV", [128, 32768], BF16))
    KT = KTV[:, 0:16384].rearrange("p (k t) -> p k t", k=2)
    V = KTV[:, 16384:32768].rearrange("p (t c) -> p t c", c=256)
    WO = KTV[:].rearrange("p (c n) -> p c n", c=16)
    with contextlib.nullcontext(st_ab) as st:
        sb = lambda name, shape, dt: st.enter_context(nc.sbuf_tensor(L + name, shape, dt))
        ps, hb_ = psum_f32(st, "pa", 4)
        QT = [sb("QT%d" % i, [128, 512], BF16) for i in range(2)]
        PT = [sb("PT%d" % i, [128, 1024], BF16) for i in range(3)]
        rden = sb("rden", [128, 512], F32)
        yst = [sb("yst%d" % i, [128, 512], F32) for i in range(2)]
        P.dma("pool", "kvwin", lambda e: e.dma_start(out=kvwin.ap().rearrange("(c r) n -> c r n", c=8),
                                                     in_=bass.AP(kvbig, jreg["j"] * (256 * 2048) + 2 * 1536 * 2048, [[1536 * 2048, 8], [2048, 768], [1, 2048]])),
              reads=["kvg%d" % c for c in range(2, 10)], writes=["kvwin"])
        for kv in range(2):
            for r in range(4):
                row0 = (1 + r) * 256 + kv * 128
                P.dma("sp", "KT", lambda e, kv=kv, r=r, row0=row0: e.dma_start(out=KT[:, kv, r * 2048:(r + 1) * 2048], in_=kvbig.ap()[row0:row0 + 128, :]), reads=["kvg0"], writes=["KT"])
        for r in range(4):
            P.dma("sp", "V", lambda e, r=r: e.dma_start(out=V[:, r * 16:(r + 1) * 16, :], in_=bass.AP(kvbig, (1536 + (1 + r) * 256) * 2048, [[256, 128], [32768, 16], [1, 256]])), reads=["kvg1"], writes=["V"])
        it = 0
        pti = 0
        for h in range(8):
            kv = h // 4
            for qg in range(4):
                qs = it % 2
                osl = it % 2
                it += 1
                O = ps[2][:, osl * 512:(osl + 1) * 512]
                Dn = ps[3][:, osl * 512:(osl + 1) * 512]
                P.dma("sp", "QT%d" % qs, lambda e, h=h, qg=qg, qs=qs: e.dma_start(out=QT[qs][:], in_=qaT.ap()[h, :, qg * 512:(qg + 1) * 512]), writes=["QT%d" % qs])

                def emit_S(kp, qs=qs, kv=kv):
                    ss_ = kp % 2
                    for u in range(2):
                        kt = 2 * kp + u
                        P.op("pe", lambda e, ss_=ss_, u=u, kt=kt: e.matmul(ps[ss_][:, u * 512:(u + 1) * 512], lhsT=KT[:, kv, kt * 128:(kt + 1) * 128], rhs=QT[qs][:], start=True, stop=True),
                             reads=["KT", "QT%d" % qs], writes=["S%d" % ss_])
                emit_S(0)
                for kp in range(32):
                    if kp + 1 < 32:
                        emit_S(kp + 1)
                    ss_ = kp % 2
                    pp = pti % 3
                    pti += 1
                    P.op("act", lambda e, ss_=ss_, pp=pp: e.activation(out=PT[pp][:], in_=ps[ss_][:], func=AF.Exp, scale=SCALE),
                         reads=["S%d" % ss_], writes=["PT%d" % pp])
                    for u in range(2):
                        kt = 2 * kp + u
                        P.op("pe", lambda e, pp=pp, u=u, kt=kt, O=O, kv=kv: e.matmul(O, lhsT=V[:, kt, kv * 128:(kv + 1) * 128], rhs=PT[pp][:, u * 512:(u + 1) * 512], start=(kt == 0), stop=(kt == 63)),
                             reads=["V", "PT%d" % pp], writes=["O%d" % osl])
                        P.op("pe", lambda e, pp=pp, u=u, kt=kt, Dn=Dn: e.matmul(Dn, lhsT=ones[:], rhs=PT[pp][:, u * 512:(u + 1) * 512], start=(kt == 0), stop=(kt == 63)),
                             reads=["ones", "PT%d" % pp], writes=["Dn%d" % osl])
                ys = osl
                P.op("dve", lambda e, Dn=Dn: e.reciprocal(out=rden[:], in_=Dn), reads=["Dn%d" % osl], writes=["rden"])
                P.op("dve", lambda e, O=O, ys=ys: e.tensor_tensor(out=yst[ys][:], in0=O, in1=rden[:], op=ALU.mult), reads=["O%d" % osl, "rden"], writes=["yst%d" % ys])
                P.dma("sp", "yst%d" % ys, lambda e, h=h, qg=qg, ys=ys: e.dma_start(out=yT.ap()[h, :, qg * 512:(qg + 1) * 512], in_=yst[ys][:]), reads=["yst%d" % ys])
        for half in range(2):
            P.dma("pool", "WO", lambda e, half=half: e.dma_start(out=WO[:, half * 8:(half + 1) * 8, :], in_=w_out.ap()[li, half * 1024:(half + 1) * 1024, :].rearrange("(c p) n -> p c n", p=128)),
                  writes=["KT", "V", "WO"])

    with contextlib.nullcontext(st_ab) as st:
        sb = lambda name, shape, dt: st.enter_context(nc.sbuf_tensor(L + name, shape, dt))
        KB = [sb("KB%d" % i, [128, 4096], BF16) for i in range(2)]
        VB = [sb("VB%d" % i, [128, 32, 128], BF16) for i in range(2)]
        QB = [sb("QB%d" % i, [128, TOK], BF16) for i in range(2)]
        MK = sb("MK", [128, 8, 17 * 128], BF16)
        ed = sb("ed", [128, 2], F32)
        NE, NPB, LA = 3, 4, 3
        E = [sb("E%d" % i, [128, 512], F32) for i in range(NE)]
        PB = [sb("PB%d" % i, [128, 512], BF16) for i in range(NPB)]
        rden = sb("rdenb", [128, 512], F32)
        yst = [sb("ystb%d" % i, [128, 512], F32) for i in range(2)]
        P.dma("sp", "MK", lambda e: e.dma_start(out=MK[:], in_=maskB.ap().rearrange("p (h m) -> p h m", h=8)), writes=["MK"])
        P.dma("sp", "ed", lambda e: e.dma_start(out=ed[:], in_=edge.ap()), writes=["ed"])

        def load_head(h):
            bs = h % 2
            for (so, tok0, ntok, wc0) in ((0, 1024, 1024, 0), (1, 0, 2048, 1024), (2, 0, 1024, 3072)):
                P.dma("sp", "KB%d" % bs, lambda e, h=h, bs=bs, so=so, tok0=tok0, ntok=ntok, wc0=wc0: e.dma_start(
                    out=KB[bs][:, wc0:wc0 + ntok],
                    in_=bass.AP(kvwin, ((h // 2) * 768 + so * 256 + (h % 2) * 128) * 2048 + tok0, [[2048, 128], [1, ntok]])),
                    reads=["kvwin"], writes=["KB%d" % bs])
                for tq in range(tok0 // 512, (tok0 + ntok) // 512):
                    wt0 = (wc0 + tq * 512 - tok0) // 128
                    P.dma("sp", "VB%d" % bs, lambda e, h=h, bs=bs, so=so, tq=tq, wt0=wt0: e.dma_start(
                        out=VB[bs][:, wt0:wt0 + 4, :],
                        in_=bass.AP(kvwin, ((4 + tq) * 768 + so * 256) * 2048 + h * 128, [[1024, 128], [131072, 4], [1, 128]])),
                        reads=["kvwin"], writes=["VB%d" % bs])
            P.dma("sp", "QB%d" % bs, lambda e, h=h, bs=bs: e.dma_start(out=QB[bs][:], in_=qbT.ap()[h, :, :]), writes=["QB%d" % bs])

        tasks = []
        for h in range(8):
            for qt in range(NT):
                for gi in range(5):
                    grp = list(range(qt + 4 * gi, min(qt + 4 * gi + 4, qt + 17)))
                    tasks.append((h, qt, gi, grp))

        def emit_S(ti):
            h, qt, gi, grp = tasks[ti]
            bs = h % 2
            sk = ti % 4
            S = hb_[sk]
            for i, wt in enumerate(grp):
                P.op("pe", lambda e, S=S, i=i, wt=wt, bs=bs, qt=qt: e.matmul(S[:, i * 128:(i + 1) * 128], lhsT=KB[bs][:, wt * 128:(wt + 1) * 128], rhs=QB[bs][:, qt * 128:(qt + 1) * 128], start=True, stop=True),
                     reads=["KB%d" % bs, "QB%d" % bs], writes=["SB%d" % sk, "S%d" % (sk // 2)])

        load_head(0)
        for ti in range(min(LA, len(tasks))):
            emit_S(ti)
        for ti, (h, qt, gi, grp) in enumerate(tasks):
            bs = h % 2
            if qt == 0 and gi == 0 and h + 1 < 8:
                load_head(h + 1)
            if ti + LA < len(tasks):
                emit_S(ti + LA)
            n = len(grp)
            sk = ti % 4
            S = hb_[sk]
            qsl = qt % 4
            osl = ((h * NT + qt) // 4) % 2
            O = ps[2][:, osl * 512 + qsl * 128: osl * 512 + (qsl + 1) * 128]
            Dn = ps[3][:, osl * 512 + qsl * 128: osl * 512 + (qsl + 1) * 128]
            es = ti % NE
            P.op("act", lambda e, S=S, es=es, n=n: e.activation(out=E[es][:, 0:n * 128], in_=S[:, 0:n * 128], func=AF.Exp, scale=SCALE),
                 reads=["SB%d" % sk], writes=["E%d" % es])
            pb = ti % NPB
            cls = ["L" if wt < 8 else ("R" if wt >= 24 else "O") for wt in grp]
            i0 = 0
            while i0 < n:
                i1 = i0
                while i1 < n and cls[i1] == cls[i0]:
                    i1 += 1
                jj0 = grp[i0] - qt
                msl = MK[:, h, jj0 * 128:(jj0 + (i1 - i0)) * 128]
                if cls[i0] == "O":
                    P.op("dve", lambda e, pb=pb, es=es, i0=i0, i1=i1, msl=msl: e.tensor_tensor(out=PB[pb][:, i0 * 128:i1 * 128], in0=E[es][:, i0 * 128:i1 * 128], in1=msl, op=ALU.mult),
                         reads=["E%d" % es, "MK"], writes=["PB%d" % pb])
                else:
                    col = 0 if cls[i0] == "L" else 1
                    P.op("dve", lambda e, pb=pb, es=es, i0=i0, i1=i1, msl=msl, col=col: e.scalar_tensor_tensor(out=PB[pb][:, i0 * 128:i1 * 128], in0=E[es][:, i0 * 128:i1 * 128], scalar=ed[:, col:col + 1], in1=msl, op0=ALU.mult, op1=ALU.mult),
                         reads=["E%d" % es, "MK", "ed"], writes=["PB%d" % pb])
                i0 = i1
            for i, wt in enumerate(grp):
                first = (wt == qt)
                lastk = (wt == qt + 16)
                P.op("pe", lambda e, O=O, pb=pb, i=i, wt=wt, bs=bs, first=first, lastk=lastk: e.matmul(O, lhsT=VB[bs][:, wt, :], rhs=PB[pb][:, i * 128:(i + 1) * 128], start=first, stop=lastk),
                     reads=["VB%d" % bs, "PB%d" % pb], writes=["O%d" % osl])
                P.op("pe", lambda e, Dn=Dn, pb=pb, i=i, first=first, lastk=lastk: e.matmul(Dn, lhsT=ones[:], rhs=PB[pb][:, i * 128:(i + 1) * 128], start=first, stop=lastk),
                     reads=["ones", "PB%d" % pb], writes=["Dn%d" % osl])
            if gi == 4 and qsl == 3:
                Of = ps[2][:, osl * 512:(osl + 1) * 512]
                Df = ps[3][:, osl * 512:(osl + 1) * 512]
                ys = osl
                qg = qt // 4
                P.op("dve", lambda e, Df=Df: e.reciprocal(out=rden[:], in_=Df), reads=["Dn%d" % osl], writes=["rdenb"])
                P.op("dve", lambda e, Of=Of, ys=ys: e.tensor_tensor(out=yst[ys][:], in0=Of, in1=rden[:], op=ALU.mult), reads=["O%d" % osl, "rdenb"], writes=["ystb%d" % ys])
                P.dma("sp", "ystb%d" % ys, lambda e, h=h, qg=qg, ys=ys: e.dma_start(out=yT.ap()[8 + h, :, qg * 512:(qg + 1) * 512], in_=yst[ys][:]), reads=["ystb%d" % ys])
        P.barrier()

    st_ab.close()
    with contextlib.ExitStack() as st:
        sb = lambda name, shape, dt: st.enter_context(nc.sbuf_tensor(L + name, shape, dt))
        ps, hb_ = psum_f32(st, "po", 3)
        yt = [sb("yt%d" % i, [128, 16, 256], F32) for i in range(2)]
        gt = [sb("gt%d" % i, [128, 16, 256], BF16) for i in range(2)]
        zt = [sb("zt%d" % i, [128, 16, 256], BF16) for i in range(2)]
        sqo = [sb("sqo%d" % i, [128, 256], F32) for i in range(2)]
        tmp = [sb("tmp%d" % i, [128, 256], F32) for i in range(2)]
        rstd = [sb("rstd%d" % i, [128, 256], F32) for i in range(2)]
        onw = sb("onw", [128, 16], F32)
        xo = [sb("xo%d" % i, [128, D], F32) for i in range(2)]
        fw = sb("fw", [128, D], F32)
        sqf = sb("sqf", [128, D], BF16)
        ssf = sb("ssf", [128, 2], F32)
        P.dma("sp", "onw", lambda e: e.dma_start(out=onw[:, 0:8], in_=bass.AP(ona, li * 1024, [[1, 128], [128, 8]]), allow_slow_non_contiguous=True), writes=["onw"])
        P.dma("sp", "onw", lambda e: e.dma_start(out=onw[:, 8:16], in_=bass.AP(onb, li * 1024, [[1, 128], [128, 8]]), allow_slow_non_contiguous=True), writes=["onw"])
        if do_final:
            P.dma("sp", "fw", lambda e: e.dma_start(out=fw[:], in_=bcast_row(fnw, 0, D)), writes=["fw"])
        oacc = 0
        for tg in range(8):
            s = tg % 2
            c0 = tg * 256
            P.dma("sp", "yt%d" % s, lambda e, s=s, c0=c0: e.dma_start(out=yt[s][:], in_=yT.ap()[:, :, c0:c0 + 256].rearrange("c d t -> d c t")), writes=["yt%d" % s])
            P.dma("sp", "gt%d" % s, lambda e, s=s, c0=c0: e.dma_start(out=gt[s][:], in_=gT.ap()[:, :, c0:c0 + 256].rearrange("c d t -> d c t")), writes=["gt%d" % s])
            for mix in range(2):
                ssp = hb_[4 + mix]
                for c in range(8):
                    ch = mix * 8 + c
                    q = ch % 2
                    P.op("act", lambda e, s=s, ch=ch, q=q: e.activation(out=sqo[q][:], in_=yt[s][:, ch, :], func=AF.Square),
                         reads=["yt%d" % s], writes=["sqo%d" % q])
                    P.op("pe", lambda e, ssp=ssp, q=q, c=c: e.matmul(ssp[:, 0:256], lhsT=onesf[:], rhs=sqo[q][:], start=(c == 0), stop=(c == 7)),
                         reads=["onesf", "sqo%d" % q], writes=["ssp%d" % mix])
                P.op("act", lambda e, ssp=ssp, mix=mix: e.activation(out=rstd[mix][:], in_=ssp[:, 0:256], func=AF.Sqrt, scale=1.0 / 1024, bias=epsb[:]),
                     reads=["ssp%d" % mix, "epsb"], writes=["rstd%d" % mix])
                P.op("dve", lambda e, mix=mix: e.reciprocal(out=rstd[mix][:], in_=rstd[mix][:]),
                     reads=["rstd%d" % mix], writes=["rstd%d" % mix])
                for c in range(8):
                    ch = mix * 8 + c
                    q = ch % 2
                    P.op("dve", lambda e, s=s, ch=ch, q=q, mix=mix: e.tensor_tensor(out=tmp[q][:], in0=yt[s][:, ch, :], in1=rstd[mix][:], op=ALU.mult),
                         reads=["yt%d" % s, "rstd%d" % mix], writes=["tmp%d" % q])
                    P.op("dve", lambda e, s=s, ch=ch, q=q: e.scalar_tensor_tensor(out=zt[s][:, ch, :], in0=gt[s][:, ch, :], scalar=onw[:, ch:ch + 1], in1=tmp[q][:], op0=ALU.mult, op1=ALU.mult),
                         reads=["gt%d" % s, "onw", "tmp%d" % q], writes=["zt%d" % s])
            for tt in range(2):
                t = tg * 2 + tt
                xs = t % 2
                P.dma("sp", "xo%d" % xs, lambda e, t=t, xs=xs: e.dma_start(out=xo[xs][:], in_=x_src.ap()[t * 128:(t + 1) * 128, :]), writes=["xo%d" % xs])
                for cg in range(4):
                    a = oacc % 4
                    oacc += 1
                    acc = hb_[a]
                    for c in range(16):
                        P.op("pe", lambda e, acc=acc, c=c, s=s, tt=tt, cg=cg: e.matmul(acc, lhsT=zt[s][:, c, tt * 128:(tt + 1) * 128], rhs=WO[:, c, cg * 512:(cg + 1) * 512], start=(c == 0), stop=(c == 15)),
                             reads=["zt%d" % s, "WO"], writes=["oacc%d" % a])
                    P.op("dve", lambda e, acc=acc, xs=xs, cg=cg: e.tensor_tensor(out=xo[xs][:, cg * 512:(cg + 1) * 512], in0=acc, in1=xo[xs][:, cg * 512:(cg + 1) * 512], op=ALU.add),
                         reads=["oacc%d" % a, "xo%d" % xs], writes=["xo%d" % xs])
                if do_final:
                    P.op("act", lambda e, xs=xs: e.activation(out=sqf[:], in_=xo[xs][:], func=AF.Square, accum_out=ssf[:, xs:xs + 1]),
                         reads=["xo%d" % xs], writes=["sqf", "ssf%d" % xs])
                    P.op("act", lambda e, xs=xs: e.activation(out=ssf[:, xs:xs + 1], in_=ssf[:, xs:xs + 1], func=AF.Sqrt, scale=1.0 / D, bias=epsb[:]),
                         reads=["ssf%d" % xs, "epsb"], writes=["ssf%d" % xs])
                    P.op("dve", lambda e, xs=xs: e.reciprocal(out=ssf[:, xs:xs + 1], in_=ssf[:, xs:xs + 1]),
                         reads=["ssf%d" % xs], writes=["ssf%d" % xs])
                    P.op("dve", lambda e, xs=xs: e.scalar_tensor_tensor(out=xo[xs][:], in0=xo[xs][:], scalar=ssf[:, xs:xs + 1], in1=fw[:], op0=ALU.mult, op1=ALU.mult),
                         reads=["xo%d" % xs, "ssf%d" % xs, "fw"], writes=["xo%d" % xs])
                P.dma("sp", "xo%d" % xs, lambda e, t=t, xs=xs: e.dma_start(out=x_dst.ap()[t * 128:(t + 1) * 128, :], in_=xo[xs][:]), reads=["xo%d" % xs])
        P.barrier()
    st_abo.close()


def _rope_tables():
    t = np.arange(SEQ)
    row = (t // 64).astype(np.float32)
    col = (t % 64).astype(np.float32)
    inv = (10000.0 ** (-np.arange(0, 64, 2, dtype=np.float32) / 64)).astype(np.float32)
    ar = row[:, None] * inv[None, :]
    ac = col[:, None] * inv[None, :]
    cr, sr, cc, sc = np.cos(ar), np.sin(ar), np.cos(ac), np.sin(ac)
    C = np.concatenate([cr, cr, cc, cc], axis=1).astype(np.float32)
    S = np.concatenate([-sr, sr, -sc, sc], axis=1).astype(np.float32)
    return C, S


def _mask_b():
    slopes = (2.0 ** -np.arange(1, 9)).astype(np.float64)
    k = np.arange(128)[:, None]
    q = np.arange(128)[None, :]
    M = np.zeros((128, 8, 17, 128), np.float64)
    for jj in range(17):
        off = (jj - 8) * 128 + k - q
        a = np.abs(off)
        mult = np.zeros_like(off, dtype=np.float64)
        for w, d in PATTERNS:
            mult += ((a <= w // 2) & (off % d == 0)).astype(np.float64)
        for h in range(8):
            M[:, h, jj, :] = mult * np.exp(-slopes[h] * a)
    return M.reshape(128, 8 * 17 * 128).astype(ml_dtypes.bfloat16)


_CACHE = {}


def _get_prog(nl, first, last):
    key = (nl, last)
    if key not in _CACHE:
        _CACHE[key] = build_program(nl, first, last)
    return _CACHE[key]


def _consts():
    if "c" not in _CACHE:
        C, S = _rope_tables()
        _CACHE["c"] = (C, S, _mask_b(), np.eye(128, dtype=np.float32).astype(ml_dtypes.bfloat16))
    return _CACHE["c"]


def run_layers(xs, layers, first, last, w):
    nl = len(layers)
    nc = _get_prog(nl, first, last)
    C, S, M, I = _consts()
    f = lambda a: np.ascontiguousarray(a, dtype=np.float32)
    sel = lambda a: f(a[layers])
    in_maps = []
    for c in range(8):
        j = c % 4
        edge = np.zeros((128, 2), np.float32)
        edge[:, 0] = 1.0 if j > 0 else 0.0
        edge[:, 1] = 1.0 if j < 3 else 0.0
        in_maps.append({
            "x": f(xs[c]), "w_in": sel(w["w_in"]), "w_out": sel(w["w_out"]), "norm_w": sel(w["norm_w"]),
            "q_norm_a": sel(w["q_norm_a"]), "k_norm_a": sel(w["k_norm_a"]),
            "out_norm_a": sel(w["out_norm_a"]), "out_norm_b": sel(w["out_norm_b"]),
            "final_norm": f(w["final_norm"]).reshape(1, D),
            "ropeC": f(C[j * TOK:(j + 1) * TOK]), "ropeS": f(S[j * TOK:(j + 1) * TOK]),
            "maskB": M, "edge": edge, "ident": I,
        })
    res = run_bass_kernel_spmd(nc, in_maps, core_ids=list(range(8)))
    return [r["out"] for r in res.results]


LAUNCH_GROUPS = [[0, 1, 2, 3]]


def kernel(x, norm_w, w_in, q_norm_a, k_norm_a, out_norm_a, out_norm_b, w_out, final_norm):
    w = dict(norm_w=np.asarray(norm_w), w_in=np.asarray(w_in), q_norm_a=np.asarray(q_norm_a), k_norm_a=np.asarray(k_norm_a),
             out_norm_a=np.asarray(out_norm_a), out_norm_b=np.asarray(out_norm_b), w_out=np.asarray(w_out), final_norm=np.asarray(final_norm))
    x = np.asarray(x)
    xs = [x[c // 4, (c % 4) * TOK:(c % 4 + 1) * TOK, :] for c in range(8)]
    ng = len(LAUNCH_GROUPS)
    for gi, grp in enumerate(LAUNCH_GROUPS):
        xs = run_layers(xs, grp, gi == 0, gi == ng - 1, w)
    out = np.empty((2, SEQ, D), np.float32)
    for c in range(8):
        out[c // 4, (c % 4) * TOK:(c % 4 + 1) * TOK, :] = xs[c]
    return out
```
